# Optimizing a Trainium2 kernel written in Bass

```python
import math
import jax, jax.numpy as jnp
from jax import lax
import numpy as np

D_MODEL = 1024
BATCH = 8
SEQ = 8192
DEPTH = 2

N_MIXERS = 2
N_A_LAYERS = (DEPTH + 1) // 2
N_B_LAYERS = DEPTH // 2

DIL_CONFIGS = ((128, 1), (512, 4), (2048, 16))
N_GROUPS = len(DIL_CONFIGS)
A_HEADS = 8
A_HEAD_DIM = 128
A_WIDTH = A_HEADS * A_HEAD_DIM
A_QKV_COLS = 3 * N_GROUPS * A_WIDTH

B_HEADS = 8
Q_LORA = 256
KV_LORA = 128
NOPE_DIM = 128
ROPE_DIM = 64
V_DIM = 128
QK_DIM = NOPE_DIM + ROPE_DIM
B_IN_COLS = Q_LORA + KV_LORA + ROPE_DIM

D_FF = 4 * D_MODEL

ROPE_THETA = 10000.0
EPS = 1e-6
Q_BLOCK = 128
NEG_FILL = -1e30

kernel_name = "hybrid_dilated_mla_encoder"


def rms_norm(x, gain):
    xf = x.astype(jnp.float32)
    y = xf * lax.rsqrt(jnp.mean(xf * xf, axis=-1, keepdims=True) + EPS)
    return (y * gain.astype(jnp.float32)).astype(x.dtype)


def rope(x, seq_len):
    d = x.shape[-1]
    pos = jnp.arange(seq_len, dtype=jnp.float32)
    freqs = ROPE_THETA ** (-jnp.arange(0, d, 2, dtype=jnp.float32) / d)
    ang = pos[:, None] * freqs[None, :]
    cos = jnp.concatenate([jnp.cos(ang), jnp.cos(ang)], -1)[:, None, :]
    sin = jnp.concatenate([jnp.sin(ang), jnp.sin(ang)], -1)[:, None, :]
    xf = x.astype(jnp.float32)
    x1, x2 = xf[..., : d // 2], xf[..., d // 2:]
    rot = jnp.concatenate([-x2, x1], -1)
    return (xf * cos + rot * sin).astype(x.dtype)


def dilated_group(q, k, v, window, dilation):
    B, S, H, dh = q.shape
    half = window // (2 * dilation)
    span = dilation * half
    s_pad = -(-S // span) * span
    L = s_pad // dilation
    nb = L // half
    scale = 1.0 / math.sqrt(dh)

    def strided(t, extra):
        t = jnp.pad(t, ((0, 0), (0, s_pad - S), (0, 0), (0, 0))).reshape(B, L, dilation, H, dh)
        return jnp.pad(t, ((0, 0), (extra, extra), (0, 0), (0, 0), (0, 0)))

    qs = strided(q, 0).reshape(B, nb, half, dilation, H, dh).transpose(1, 0, 2, 3, 4, 5)
    ks = strided(k, half)
    vs = strided(v, half)
    qi = jnp.arange(half)[:, None]
    kj = jnp.arange(3 * half)[None, :]
    res = jnp.arange(dilation)[:, None, None]
    band = jnp.abs(qi + half - kj) <= half

    def block(args):
        n, qb = args
        kb = lax.dynamic_slice_in_dim(ks, n * half, 3 * half, axis=1)
        vb = lax.dynamic_slice_in_dim(vs, n * half, 3 * half, axis=1)
        m_k = (n - 1) * half + kj
        pos_k = m_k[None] * dilation + res
        valid = band[None] & (m_k[None] >= 0) & (pos_k < S)
        s = jnp.einsum('bqchd,bkchd->bchqk', qb, kb).astype(jnp.float32) * scale
        s = jnp.where(valid[None, :, None], s, NEG_FILL)
        mx = jnp.max(s, axis=-1, keepdims=True)
        p = jnp.exp(s - mx)
        den = jnp.sum(p, axis=-1)
        o = jnp.einsum('bchqk,bkchd->bqchd', p.astype(vb.dtype), vb).astype(jnp.float32)
        o = o / den.transpose(0, 3, 1, 2)[..., None]
        lse = (mx[..., 0] + jnp.log(den)).transpose(0, 3, 1, 2)
        return o, lse

    o, lse = lax.map(block, (jnp.arange(nb), qs))
    o = o.transpose(1, 0, 2, 3, 4, 5).reshape(B, s_pad, H, dh)[:, :S]
    lse = lse.transpose(1, 0, 2, 3, 4).reshape(B, s_pad, H)[:, :S]
    return o, lse


def dilated_mixer(h, w_qkv, q_gain, k_gain, w_o):
    B, S, _ = h.shape
    qkv = (h @ w_qkv).reshape(B, S, 3, N_GROUPS, A_HEADS, A_HEAD_DIM)
    outs, lses = [], []
    for g, (window, dil) in enumerate(DIL_CONFIGS):
        q = rope(rms_norm(qkv[:, :, 0, g], q_gain[g]), S)
        k = rope(rms_norm(qkv[:, :, 1, g], k_gain[g]), S)
        v = qkv[:, :, 2, g]
        o, lse = dilated_group(q, k, v, window, dil)
        outs.append(o)
        lses.append(lse)
    wts = jax.nn.softmax(jnp.stack(lses, 0), axis=0)
    o = wts[0][..., None] * outs[0]
    for g in range(1, N_GROUPS):
        o = o + wts[g][..., None] * outs[g]
    return o.reshape(B, S, A_WIDTH).astype(h.dtype) @ w_o


def mla_mixer(h, w_in, q_a_gain, w_qb, kv_a_gain, w_kvb, q_gain, k_gain, w_o):
    B, S, _ = h.shape
    lat = h @ w_in
    c_q = lat[..., :Q_LORA]
    c_kv = lat[..., Q_LORA:Q_LORA + KV_LORA]
    k_rope = lat[..., Q_LORA + KV_LORA:]
    q = (rms_norm(c_q, q_a_gain) @ w_qb).reshape(B, S, B_HEADS, QK_DIM)
    kv = (rms_norm(c_kv, kv_a_gain) @ w_kvb).reshape(B, S, B_HEADS, NOPE_DIM + V_DIM)
    k_nope, v = kv[..., :NOPE_DIM], kv[..., NOPE_DIM:]
    k = jnp.concatenate(
        [k_nope, jnp.broadcast_to(k_rope[:, :, None, :], (B, S, B_HEADS, ROPE_DIM))], -1)
    q = rms_norm(q, q_gain)
    k = rms_norm(k, k_gain)
    q = jnp.concatenate([q[..., :NOPE_DIM], rope(q[..., NOPE_DIM:], S)], -1)
    k = jnp.concatenate([k[..., :NOPE_DIM], rope(k[..., NOPE_DIM:], S)], -1)
    scale = 1.0 / math.sqrt(QK_DIM)
    nblk = S // Q_BLOCK
    qb = q.reshape(B, nblk, Q_BLOCK, B_HEADS, QK_DIM).transpose(1, 0, 2, 3, 4)

    def attend(qblk):
        s = jnp.einsum('bqhd,bkhd->bhqk', qblk, k).astype(jnp.float32) * scale
        p = jax.nn.softmax(s, axis=-1)
        return jnp.einsum('bhqk,bkhd->bqhd', p.astype(v.dtype), v)

    o = lax.map(attend, qb)
    o = o.transpose(1, 0, 2, 3, 4).reshape(B, S, B_HEADS * V_DIM)
    return o @ w_o


def sq_relu_mlp(h, w1, w2):
    a = jax.nn.relu(h @ w1)
    return (a * a) @ w2


def setup_inputs(seed: int = 0) -> dict:
    key = jax.random.key(seed)
    ks = jax.random.split(key, 16)

    def w(k, shape, fan_in):
        return jax.random.normal(k, shape, jnp.float32) * (fan_in ** -0.5)

    def gain(k, shape):
        return 1.0 + 0.05 * jax.random.normal(k, shape, jnp.float32)

    return {
        "x": jax.random.normal(ks[0], (BATCH, SEQ, D_MODEL), jnp.float32),
        "norm_mix": gain(ks[1], (DEPTH, D_MODEL)),
        "norm_ffn": gain(ks[2], (DEPTH, D_MODEL)),
        "a_w_qkv": w(ks[3], (N_A_LAYERS, D_MODEL, A_QKV_COLS), D_MODEL),
        "a_q_gain": gain(ks[4], (N_A_LAYERS, N_GROUPS, A_HEAD_DIM)),
        "a_k_gain": gain(ks[5], (N_A_LAYERS, N_GROUPS, A_HEAD_DIM)),
        "a_w_o": w(ks[6], (N_A_LAYERS, A_WIDTH, D_MODEL), A_WIDTH),
        "b_w_in": w(ks[7], (N_B_LAYERS, D_MODEL, B_IN_COLS), D_MODEL),
        "b_q_a_gain": gain(ks[8], (N_B_LAYERS, Q_LORA)),
        "b_w_qb": w(ks[9], (N_B_LAYERS, Q_LORA, B_HEADS * QK_DIM), Q_LORA),
        "b_kv_a_gain": gain(ks[10], (N_B_LAYERS, KV_LORA)),
        "b_w_kvb": w(ks[11], (N_B_LAYERS, KV_LORA, B_HEADS * (NOPE_DIM + V_DIM)), KV_LORA),
        "b_q_gain": gain(ks[12], (N_B_LAYERS, QK_DIM)),
        "b_k_gain": gain(ks[13], (N_B_LAYERS, QK_DIM)),
        "b_w_o": w(ks[14], (N_B_LAYERS, B_HEADS * V_DIM, D_MODEL), B_HEADS * V_DIM),
        "ffn_w1": w(jax.random.fold_in(ks[15], 0), (DEPTH, D_MODEL, D_FF), D_MODEL),
        "ffn_w2": w(jax.random.fold_in(ks[15], 1), (DEPTH, D_FF, D_MODEL), D_FF),
    }


def reference(x, norm_mix, norm_ffn, a_w_qkv, a_q_gain, a_k_gain, a_w_o,
              b_w_in, b_q_a_gain, b_w_qb, b_kv_a_gain, b_w_kvb, b_q_gain, b_k_gain, b_w_o,
              ffn_w1, ffn_w2):
    for i in range(DEPTH):
        h = rms_norm(x, norm_mix[i])
        j = i // N_MIXERS
        if i % N_MIXERS == 0:
            y = dilated_mixer(h, a_w_qkv[j], a_q_gain[j], a_k_gain[j], a_w_o[j])
        else:
            y = mla_mixer(h, b_w_in[j], b_q_a_gain[j], b_w_qb[j], b_kv_a_gain[j],
                          b_w_kvb[j], b_q_gain[j], b_k_gain[j], b_w_o[j])
        x = x + y
        h = rms_norm(x, norm_ffn[i])
        x = x + sq_relu_mlp(h, ffn_w1[i], ffn_w2[i])
    return x
```

```python
import contextlib
import numpy as np
import ml_dtypes
import concourse.bass as bass
import concourse.mybir as mybir
from concourse.bass_utils import run_bass_kernel_spmd

F32 = mybir.dt.float32
BF16 = mybir.dt.bfloat16
AF = mybir.ActivationFunctionType
ALU = mybir.AluOpType

S = 8192
D = 1024
EPS = 1e-6
DIL = (1, 4, 16)
NCORES = 8
PSUM_NAMES = frozenset(("pA", "pB", "pS", "pR", "pX", "pO", "pD"))
import os as _os0
POOL2DVE = _os0.environ.get('K_POOL2DVE') == '1'

V_NMIX = (0, 16)
V_NFFN = (8, 24)
V_AQ = 32
V_AK = 35
V_QA = 38
V_KVA = 40
V_QG = 41
V_QGR = 42
V_KG = 43
V_KGR = 44
V_E0 = 47
NV = 48
CB_PMA = 0
CB_PMB = 128
CB_MK = 192
CB_MKN2 = 960
NCB = 1472


class _Op:
    __slots__ = ("eng", "idx", "fn", "deps", "is_dma", "dma_key", "signal", "sem", "val", "waits")

    def __init__(self, eng, idx, fn, is_dma, dma_key):
        self.eng = eng
        self.idx = idx
        self.fn = fn
        self.deps = []
        self.is_dma = is_dma
        self.dma_key = dma_key
        self.signal = False
        self.sem = None
        self.val = None
        self.waits = []


class Tk:
    ENGS = ("pe", "act", "dve", "pool", "sp")
    EPOCH = 4000

    def __init__(self, nc):
        self.nc = nc
        self.ops = {e: [] for e in self.ENGS}
        self.last_w = {}
        self.readers = {}
        self.nsem = 0

    def _newsem(self, name):
        self.nsem += 1
        return self.nc.alloc_semaphore(name=f"{self.pfx}s{self.nsem}_{name}")

    def op(self, eng, fn, r=(), w=(), dma_key=None):
        is_dma = dma_key is not None
        if eng == "pool" and not is_dma and POOL2DVE:
            eng = "dve"
        o = _Op(eng, len(self.ops[eng]), fn, is_dma, dma_key)
        deps = []
        for k in r:
            lw = self.last_w.get(k)
            if lw is not None:
                deps.append((lw, 0))
            if isinstance(k, tuple) and k[0] in PSUM_NAMES:
                for rd in self.readers.get(k, ()):
                    if rd.eng != eng:
                        deps.append((rd, 1))
        for k in w:
            lw = self.last_w.get(k)
            if lw is not None:
                deps.append((lw, 1))
            for rd in self.readers.get(k, ()):
                deps.append((rd, 1))
        for k in w:
            self.last_w[k] = o
            self.readers[k] = []
        for k in r:
            self.readers.setdefault(k, []).append(o)
        best = {}
        for d, kind in deps:
            if d is o:
                continue
            if d.is_dma:
                best[("dma", id(d))] = d
                continue
            if d.eng == eng and not is_dma:
                if kind != 0 or eng == "pe" or (o.idx - d.idx) > 2:
                    continue
            cur = best.get(d.eng)
            if cur is None or d.idx > cur.idx:
                best[d.eng] = d
        o.deps = list(best.values())
        for d in o.deps:
            d.signal = True
        self.ops[eng].append(o)
        return o

    def fence(self, keys):
        o = self.op("sp", None, r=list(keys))
        last = {}
        for eng in self.ENGS:
            for d in self.ops[eng]:
                if d.is_dma:
                    last[d.dma_key] = d
        have = set(id(d) for d in o.deps)
        for d in last.values():
            if id(d) not in have:
                o.deps.append(d)
                d.signal = True

    def finalize(self):
        dma_sems = {}
        dma_cnt = {}
        for eng in self.ENGS:
            cur_sem = None
            cnt = 0
            for o in self.ops[eng]:
                if o.is_dma:
                    k = o.dma_key
                    if k not in dma_sems:
                        dma_sems[k] = self._newsem("d")
                        dma_cnt[k] = 0
                    dma_cnt[k] += 16
                    o.sem = dma_sems[k]
                    o.val = dma_cnt[k]
                    o.signal = True
                elif o.signal:
                    if cur_sem is None or cnt >= self.EPOCH:
                        cur_sem = self._newsem(eng)
                        cnt = 0
                    cnt += 1
                    o.sem = cur_sem
                    o.val = cnt
        for eng in self.ENGS:
            waited = {}
            for o in self.ops[eng]:
                ws = {}
                for d in o.deps:
                    sid = id(d.sem)
                    if waited.get(sid, -1) >= d.val:
                        continue
                    if sid not in ws or ws[sid][1] < d.val:
                        ws[sid] = (d.sem, d.val)
                for sid, (s, v) in ws.items():
                    waited[sid] = v
                o.waits = list(ws.values())

    def emit(self, block):
        self.finalize()
        handles = {"pe": block.tensor, "act": block.scalar, "dve": block.vector,
                   "pool": block.gpsimd, "sp": block.sync}
        for eng in self.ENGS:
            ops = self.ops[eng]
            if not ops:
                continue

            def body(e, ops=ops):
                for o in ops:
                    for (s, v) in o.waits:
                        e.wait_ge(s, v)
                    ins = o.fn(e) if o.fn is not None else None
                    if o.signal:
                        ins.then_inc(o.sem, 16 if o.is_dma else 1)

            handles[eng](body)


class Ctx:
    pass


def run_phase(nc, body, pfx):
    with nc.cleanup_on_exit():
        with contextlib.ExitStack() as st:
            T = Tk(nc)
            T.pfx = pfx
            T.sb = lambda name, shape, dt: st.enter_context(nc.sbuf_tensor(pfx + name, shape, dt))
            T.ps = lambda name: st.enter_context(nc.psum_tensor(pfx + name, [128, 512], F32))
            T.ps2 = lambda name: st.enter_context(nc.psum_tensor(pfx + name, [128, 1024], F32))
            body(T)
            with nc.Block() as block:
                T.emit(block)
        nc.all_engine_barrier()


def mm(T, out, lhsT, rhs, start, stop, r, w):
    T.op("pe", lambda e: e.matmul(out, lhsT=lhsT, rhs=rhs, start=start, stop=stop), r=r, w=w)


def dma(T, out, in_, r, w, key, eng="sp"):
    T.op(eng, lambda e: e.dma_start(out=out, in_=in_), r=r, w=w, dma_key=key)


def load_consts(T, C):
    vec = T.sb("vec", [128, NV], F32)
    cst = T.sb("cst", [128, NCB], BF16)
    ones_b = T.sb("ones_b", [128, 128], BF16)
    dma(T, vec[:], C.vecs, [], ["vec"], "vec")
    dma(T, cst[:], C.cb16, [], ["cst"], "cst")
    T.op("dve", lambda e: e.memset(ones_b[:], 1.0), w=["ones_b"])
    return vec, cst, None, ones_b


def norm_prologue(T, C, xsrc, t0, TS, gcol0, xin, sqx, hT, rbc, stp, vec, ones_b):
    ntb = TS // 512
    for kc in range(8):
        b = kc % 2
        dma(T, xin[b][:], xsrc[kc * 128:(kc + 1) * 128, t0:t0 + TS], [("dr", "xsrc")], [("xin", b)], ("xin", b))
        T.op("act", lambda e, b=b: e.activation(out=sqx[b][:], in_=xin[b][:], func=AF.Square),
             r=[("xin", b)], w=[("sqx", b)])
        for tb in range(ntb):
            mm(T, stp[tb][0][:], ones_b[:], sqx[b][:, tb * 512:(tb + 1) * 512], kc == 0, kc == 7,
               ["ones_b", ("sqx", b)], [stp[tb][1]])
    for tb in range(ntb):
        sl = slice(tb * 512, (tb + 1) * 512)
        T.op("act", lambda e, tb=tb, sl=sl: e.activation(out=rbc[:, sl], in_=stp[tb][0][:], func=AF.Ln,
                                                         scale=1.0 / D, bias=EPS),
             r=[stp[tb][1]], w=["rbc"])
    T.op("act", lambda e: e.activation(out=rbc[:], in_=rbc[:], func=AF.Exp, scale=-0.5), r=["rbc"], w=["rbc"])
    for kc in range(8):
        b = kc % 2
        dma(T, xin[b][:], xsrc[kc * 128:(kc + 1) * 128, t0:t0 + TS], [("dr", "xsrc")], [("xin", b)], ("xin", b))
        T.op("dve", lambda e, b=b, kc=kc: e.scalar_tensor_tensor(out=hT[:, kc, :], in0=xin[b][:],
                                                              scalar=vec[:, gcol0 + kc:gcol0 + kc + 1], in1=rbc[:],
                                                              op0=ALU.mult, op1=ALU.mult),
             r=[("xin", b), "vec", "rbc"], w=["hT"])


def rstd_from_ss(T, pss, psk, rs, rsk, scale, bias_ap, bias_key):
    T.op("act", lambda e: e.activation(out=rs, in_=pss, func=AF.Ln, scale=scale, bias=EPS), r=[psk], w=[rsk])
    T.op("act", lambda e: e.activation(out=rs, in_=rs, func=AF.Exp, scale=-0.5), r=[rsk], w=[rsk])


def phase_cast(T, C):
    keys = []
    for i in range(8):
        dma(T, C.wqkv_b[i * 128:(i + 1) * 128, :], C.a_w_qkv[i * 128:(i + 1) * 128, :], [], [("dr", "wqkv", i)], "cast",
            eng="pool")
        keys.append(("dr", "wqkv", i))
    T.fence(keys)


def other_casts(T, C):
    for ci_, (dst, src, rows, piece) in enumerate(C.casts):
        for i in range(0, rows, piece):
            dma(T, dst[i:i + piece, :], src[i:i + piece, :], [], [("dr", "wcast", ci_, i)], "cast", eng="pool")


def phase_a1(T, C):
    TS = 2048
    vec, cst, ones_f, ones_b = load_consts(T, C)
    wq_keys = []
    cast_keys = []
    for i in range(0):
        dma(T, C.wqkv_b[i * 128:(i + 1) * 128, :], C.a_w_qkv[i * 128:(i + 1) * 128, :], [], [("dr", "wqkv", i)], "castq",
            eng="pool")
        wq_keys.append(("dr", "wqkv", i))
    lvl = getattr(C, "dbg_lvl", 0)
    if lvl == 1:
        T.fence(cast_keys + wq_keys)
        return
    xin = [T.sb(f"xin{i}", [128, TS], F32) for i in range(2)]
    sqx = [T.sb(f"sqx{i}", [128, TS], BF16) for i in range(2)]
    hT = T.sb("hT", [128, 8, TS], BF16)
    rbc = T.sb("rbc", [128, TS], F32)
    cosb = T.sb("cosb", [128, TS], F32)
    sinb = T.sb("sinb", [128, TS], F32)
    wb = [T.sb(f"wb{i}", [128, 8, 512], BF16) for i in range(3)]
    sqb = [T.sb(f"sqb{i}", [128, 512], BF16) for i in range(2)]
    gub = [T.sb(f"gub{i}", [128, 512], BF16) for i in range(2)]
    t1 = [T.sb(f"t1{i}", [128, 512], F32) for i in range(2)]
    t2 = [T.sb(f"t2{i}", [128, 512], F32) for i in range(2)]
    rs = [T.sb(f"rs{i}", [128, 512], F32) for i in range(2)]
    qst = [T.sb(f"qst{i}", [128, TS], BF16) for i in range(2)]
    vst = [T.sb(f"vst{i}", [128, 512], BF16) for i in range(3)]
    pA = [T.ps(f"pA{i}") for i in range(2)]
    pS = [T.ps(f"pS{i}") for i in range(2)]
    pR = [T.ps(f"pR{i}") for i in range(2)]
    pX = [T.ps(f"pX{i}") for i in range(2)]
    stp = [(pX[0], ("pX", 0)), (pX[1], ("pX", 1)), (pS[0], ("pS", 0)), (pS[1], ("pS", 1))]
    wq_v = C.wqkv_b.rearrange("(k p) n -> p k n", p=128)
    pma = cst[:, CB_PMA:CB_PMA + 128]
    cnt = [0, 0, 0]

    for sbi in range(S // TS):
        t0 = sbi * TS
        norm_prologue(T, C, C.xT, t0, TS, V_NMIX[0], xin, sqx, hT, rbc, stp, vec, ones_b)
        dma(T, cosb[:], C.cosA[:, t0:t0 + TS], [], ["cosb"], "cosb")
        dma(T, sinb[:], C.sinA[:, t0:t0 + TS], [], ["sinb"], "sinb")
        if lvl == 2:
            T.fence(cast_keys + wq_keys)
            return

        groups = []
        for g in range(3):
            for qk in range(2):
                for hh in range(2):
                    groups.append(("qk", g, qk, hh))
        for g in range(3):
            for cbk in range(2):
                groups.append(("v", g, 2, cbk))

        def gcol(gr):
            kind, g, qk, hh = gr
            return (qk * 3 + g) * 1024 + hh * 512

        def load_w(gi):
            i = gi % 3
            c0 = gcol(groups[gi])
            dma(T, wb[i][:], wq_v[:, :, c0:c0 + 512], wq_keys, [("wb", i)], ("wb", i))

        items = []
        for gi, gr in enumerate(groups):
            kind, g, qk, hh = gr
            r_ = DIL[g]
            if kind == "qk":
                for hl in range(4):
                    for tb in range(4):
                        items.append((gi, "qk", g, qk, hh * 4 + hl, hl, tb))
            else:
                for tile in range(16):
                    items.append((gi, "v", g, hh, tile, 0, 0))

        load_w(0)
        load_w(1)
        loaded = [1]

        def emit_proj(item, it):
            gi = item[0]
            i = gi % 3
            while loaded[0] < min(gi + 2, len(groups) - 1):
                loaded[0] += 1
                load_w(loaded[0])
            pa = pA[it % 2]
            if item[1] == "qk":
                _, _, g, qk, h, hl, tb = item
                for kc in range(8):
                    mm(T, pa[:], wb[i][:, kc, hl * 128:(hl + 1) * 128], hT[:, kc, tb * 512:(tb + 1) * 512],
                       kc == 0, kc == 7, [("wb", i), "hT"], [("pA", it % 2)])
            else:
                _, _, g, cbk, tile, _, _ = item
                r_ = DIL[g]
                tpr = 16 // r_
                c, m_ = divmod(tile, tpr)
                for kc in range(8):
                    hv = hT[:, kc, :].rearrange("p (m c) -> p c m", c=r_)
                    mm(T, pa[:], hv[:, c, m_ * 128:(m_ + 1) * 128], wb[i][:, kc, :],
                       kc == 0, kc == 7, [("wb", i), "hT"], [("pA", it % 2)])

        def emit_post(item, it):
            j = it % 2
            pa = pA[j]
            if item[1] == "qk":
                _, _, g, qk, h, hl, tb = item
                r_ = DIL[g]
                gc = (V_AQ if qk == 0 else V_AK) + g
                gap = vec[:, gc:gc + 1]
                sl = slice(tb * 512, (tb + 1) * 512)
                PLV = int(_os0.environ.get("K_POSTLV", "99"))
                T.op("act", lambda e: e.activation(out=sqb[j][:], in_=pa[:], func=AF.Square),
                     r=[("pA", j)], w=[("sqb", j)])
                if PLV < 2:
                    return
                T.op("act", lambda e: e.activation(out=gub[j][:], in_=pa[:], func=AF.Copy, scale=gap),
                     r=[("pA", j), "vec"], w=[("gub", j)])
                if PLV < 3:
                    return
                mm(T, pS[j][:], ones_b[:], sqb[j][:], True, True, ["ones_b", ("sqb", j)], [("pS", j)])
                mm(T, pR[j][:], pma, gub[j][:], True, True, ["cst", ("gub", j)], [("pR", j)])
                if PLV < 4:
                    return
                VAR = _os0.environ.get("K_VAR", "")
                if VAR == "A":
                    T.op("dve", lambda e: e.scalar_tensor_tensor(out=t1[j][:], in0=pa[:], scalar=gap, in1=rbc[:, sl],
                                                                 op0=ALU.mult, op1=ALU.mult),
                         r=[("pA", j), "vec", "rbc"], w=[("t1", j)])
                elif VAR == "B":
                    T.op("dve", lambda e: e.scalar_tensor_tensor(out=t1[j][:], in0=pa[:], scalar=1.5, in1=cosb[:, sl],
                                                                 op0=ALU.mult, op1=ALU.mult),
                         r=[("pA", j), "cosb"], w=[("t1", j)])
                elif VAR == "C":
                    T.op("dve", lambda e: e.tensor_tensor(out=t1[j][:], in0=pa[:], in1=cosb[:, sl], op=ALU.mult),
                         r=[("pA", j), "cosb"], w=[("t1", j)])
                else:
                    T.op("dve", lambda e: e.scalar_tensor_tensor(out=t1[j][:], in0=pa[:], scalar=gap, in1=cosb[:, sl],
                                                                 op0=ALU.mult, op1=ALU.mult),
                         r=[("pA", j), "vec", "cosb"], w=[("t1", j)])
                if PLV < 5:
                    return
                T.op("dve", lambda e: e.tensor_tensor(out=t2[j][:], in0=pR[j][:], in1=sinb[:, sl], op=ALU.mult),
                     r=[("pR", j), "sinb"], w=[("t2", j)])
                if PLV < 6:
                    return
                T.op("dve", lambda e: e.tensor_tensor(out=t1[j][:], in0=t1[j][:], in1=t2[j][:], op=ALU.add),
                     r=[("t1", j), ("t2", j)], w=[("t1", j)])
                if PLV < 7:
                    return
                rstd_from_ss(T, pS[j][:], ("pS", j), rs[j][:], ("rs", j), 1.0 / 128, None, None)
                if PLV < 8:
                    return
                ch = cnt[1]
                qs = qst[ch % 2]
                qv = qs[:].rearrange("p (c m) -> p c m", c=r_)
                w_ = 512 // r_
                T.op("pool", lambda e: e.tensor_tensor(out=qv[:, :, tb * w_:(tb + 1) * w_],
                                                       in0=t1[j][:].rearrange("p (m c) -> p c m", c=r_),
                                                       in1=rs[j][:].rearrange("p (m c) -> p c m", c=r_), op=ALU.mult),
                     r=[("t1", j), ("rs", j)], w=[("qst", ch % 2)])
                if tb == 3:
                    dst = C.qk_a[qk, g, h].rearrange("p (c l) -> p c l", c=r_)[:, :, sbi * (TS // r_):(sbi + 1) * (TS // r_)]
                    dma(T, dst, qv, [("qst", ch % 2)], [("dr", "qk_a")], ("qst", ch % 2))
                    cnt[1] += 1
            else:
                _, _, g, cbk, tile, _, _ = item
                r_ = DIL[g]
                L = S // r_
                tpr = 16 // r_
                c, m_ = divmod(tile, tpr)
                vi = cnt[2] % 3
                cnt[2] += 1
                col = g * 16 + tile
                T.op("act", lambda e: e.activation(out=vst[vi][:], in_=pa[:], func=AF.Copy),
                     r=[("pA", j)], w=[("vst", vi)])
                row0 = c * L + sbi * (TS // r_) + m_ * 128
                dma(T, C.va[g, row0:row0 + 128, cbk * 512:(cbk + 1) * 512], vst[vi][:], [("vst", vi)], [("dr", "va")],
                    ("vst", vi))

        base = cnt[0]
        if lvl >= 3:
            items = items[:lvl]
        n = len(items)
        import os as _os
        nopost = _os.environ.get("K_NOPOST") == "1"
        emit_proj(items[0], base)
        for ii in range(n):
            if ii + 1 < n:
                emit_proj(items[ii + 1], base + ii + 1)
            if not nopost:
                emit_post(items[ii], base + ii)
        cnt[0] += n
        if lvl >= 3:
            break
    T.fence([("dr", "qk_a"), ("dr", "va")] + cast_keys + wq_keys)


def phase_a2(T, C):
    vec, cst, ones_f, ones_b = load_consts(T, C)
    BT = 2048
    qb = [T.sb(f"qb{i}", [128, BT], BF16) for i in range(2)]
    kb = [T.sb(f"kb{i}", [128, BT + 16 * 128], BF16) for i in range(2)]
    vb = [T.sb(f"vb{i}", [128, 32, 128], BF16) for i in range(2)]
    num = [T.sb(f"num{i}", [128, BT], F32) for i in range(2)]
    den = [T.sb(f"den{i}", [128, BT], F32) for i in range(2)]
    pT = [T.sb(f"pT{i}", [128, 512], BF16) for i in range(4)]
    ost = [T.sb(f"ost{i}", [128, BT], BF16) for i in range(2)]
    pS = [T.ps(f"pS{i}") for i in range(4)]
    pO = [T.ps(f"pO{i}") for i in range(2)]
    pD = [T.ps(f"pD{i}") for i in range(2)]
    for i in range(2):
        T.op("pool", lambda e, i=i: e.memset(kb[i][:], 0.0), w=[("kb", i)])
        T.op("pool", lambda e, i=i: e.memset(vb[i][:], 0.0), w=[("vb", i)])
    other_casts(T, C)
    scale = 1.0 / np.sqrt(128.0)
    steps = [(h, nb, g) for h in range(8) for nb in range(4) for g in range(3)]

    def views(si):
        h, nb, g = steps[si]
        i = si % 2
        r_ = DIL[g]
        W = BT // r_ + 128
        NT = (BT // r_) // 128 + 1
        qv = qb[i][:].rearrange("p (c m) -> p c m", c=r_)
        kv = kb[i][:, 0:r_ * W].rearrange("p (c w) -> p c w", c=r_)
        vv = vb[i][:, 0:r_ * NT, :].rearrange("p (c j) d -> p c j d", c=r_)
        return qv, kv, vv

    def load(si):
        h, nb, g = steps[si]
        i = si % 2
        r_ = DIL[g]
        L = S // r_
        n_ = BT // r_
        NT = n_ // 128 + 1
        m0 = nb * n_
        qv, kv, vv = views(si)
        qsrc = C.qk_a[0, g, h].rearrange("p (c l) -> p c l", c=r_)
        ksrc = C.qk_a[1, g, h].rearrange("p (c l) -> p c l", c=r_)
        dma(T, qv, qsrc[:, :, m0:m0 + n_], [("dr", "qk_a")], [("qb", i)], ("qb", i))
        lo = m0 - 64
        hi = m0 + n_ + 64
        clo = max(lo, 0)
        chi = min(hi, L)
        dma(T, kv[:, :, clo - lo:chi - lo], ksrc[:, :, clo:chi], [("dr", "qk_a")], [("kb", i)], ("kb", i))
        vsrc = C.va[g].rearrange("(c l) n -> c l n", c=r_)
        hc = slice(h * 128, (h + 1) * 128)
        jj0 = 1 if lo < 0 else 0
        jj1 = NT - 1 if hi > L else NT
        if r_ > 1:
            for c in range(r_):
                src = vsrc[c, lo + 128 * jj0:lo + 128 * jj1, hc].rearrange("(j i) d -> i j d", i=128)
                dma(T, vv[:, c, jj0:jj1, :], src, [("dr", "va")], [("vb", i)], ("vb", i))
        else:
            jm = (jj0 + jj1) // 2
            for a0, a1 in ((jj0, jm), (jm, jj1)):
                src = vsrc[0, lo + 128 * a0:lo + 128 * a1, hc].rearrange("(j i) d -> i j d", i=128)
                dma(T, vv[:, 0, a0:a1, :], src, [("dr", "va")], [("vb", i)], ("vb", i))
        if lo < 0:
            src = vsrc[:, 0:64, hc].rearrange("c i d -> i c d")
            dma(T, vv[64:128, :, 0, :], src, [("dr", "va")], [("vb", i)], ("vb", i))
        if hi > L:
            src = vsrc[:, L - 64:L, hc].rearrange("c i d -> i c d")
            dma(T, vv[0:64, :, NT - 1, :], src, [("dr", "va")], [("vb", i)], ("vb", i))

    pairs = []
    for si in range(len(steps)):
        for p in range(8):
            pairs.append((si, p))

    def chunk_info(si, ci):
        h, nb, g = steps[si]
        r_ = DIL[g]
        QC = (BT // r_) // 128
        c, a = divmod(ci, QC)
        var = 0
        if nb == 0 and a == 0:
            var = 1
        elif nb == 3 and a == QC - 1:
            var = 2
        return c, a, var

    loaded = [-1]

    def ensure_loaded(si):
        while loaded[0] < min(si + 1, len(steps) - 1):
            loaded[0] += 1
            load(loaded[0])

    def emit_s(pi):
        si, p = pairs[pi]
        i = si % 2
        qv, kv, vv = views(si)
        ps = pS[pi % 4]
        for s_ in range(2):
            c, a, var = chunk_info(si, 2 * p + s_)
            q_ap = qv[:, c, 128 * a:128 * a + 128]
            mm(T, ps[:, s_ * 256:s_ * 256 + 128], kv[:, c, 128 * a:128 * a + 128], q_ap, True, True,
               [("kb", i), ("qb", i)], [("pS", pi % 4)])
            mm(T, ps[:, s_ * 256 + 128:s_ * 256 + 256], kv[:, c, 128 * a + 128:128 * a + 256], q_ap, True, True,
               [("kb", i), ("qb", i)], [("pS", pi % 4)])

    def emit_rest(pi):
        si, p = pairs[pi]
        h, nb, g = steps[si]
        i = si % 2
        r_ = DIL[g]
        qv, kv, vv = views(si)
        ps = pS[pi % 4]
        pt = pT[pi % 4]
        ptk = ("pT", pi % 4)
        T.op("act", lambda e: e.activation(out=pt[:], in_=ps[:], func=AF.Exp, scale=float(scale)),
             r=[("pS", pi % 4)], w=[ptk])
        infos = [chunk_info(si, 2 * p + s_) for s_ in range(2)]
        if infos[0][2] == 0 and infos[1][2] == 0:
            T.op("dve", lambda e: e.tensor_tensor(out=pt[:], in0=pt[:], in1=cst[:, CB_MKN2:CB_MKN2 + 512], op=ALU.mult),
                 r=[ptk, "cst"], w=[ptk])
        else:
            for s_ in range(2):
                mo = CB_MK + 256 * infos[s_][2]
                T.op("dve", lambda e, s_=s_, mo=mo: e.tensor_tensor(out=pt[:, s_ * 256:(s_ + 1) * 256],
                                                                   in0=pt[:, s_ * 256:(s_ + 1) * 256],
                                                                   in1=cst[:, mo:mo + 256], op=ALU.mult),
                     r=[ptk, "cst"], w=[ptk])
        for s_ in range(2):
            ci = 2 * p + s_
            c, a, var = infos[s_]
            col = (ci % 4) * 128
            ob = (si * 4 + ci // 4) % 2
            po = pO[ob]
            pd = pD[ob]
            mm(T, po[:, col:col + 128], vv[:, c, a, :], pt[:, s_ * 256:s_ * 256 + 128], True, False,
               [("vb", i), ptk], [("pO", ob)])
            mm(T, po[:, col:col + 128], vv[:, c, a + 1, :], pt[:, s_ * 256 + 128:s_ * 256 + 256], False, True,
               [("vb", i), ptk], [("pO", ob)])
            mm(T, pd[:, col:col + 128], ones_b[:], pt[:, s_ * 256:s_ * 256 + 128], True, False,
               ["ones_b", ptk], [("pD", ob)])
            mm(T, pd[:, col:col + 128], ones_b[:], pt[:, s_ * 256 + 128:s_ * 256 + 256], False, True,
               ["ones_b", ptk], [("pD", ob)])
        if p % 2 == 1:
            ci0 = 2 * p - 2
            ob = (si * 4 + ci0 // 4) % 2
            nbuf = (h * 4 + nb) % 2
            QC = (BT // r_) // 128
            c0, a0 = divmod(ci0, QC)
            for (acc, pp, ak, pk) in ((num[nbuf], pO[ob], ("num", nbuf), ("pO", ob)),
                                      (den[nbuf], pD[ob], ("den", nbuf), ("pD", ob))):
                av = acc[:].rearrange("p (m c) -> p c m", c=r_)
                if r_ == 16:
                    dst = av[:, c0:c0 + 4, 0:128]
                    src = pp[:].rearrange("p (c m) -> p c m", c=4)
                else:
                    dst = av[:, c0, 128 * a0:128 * a0 + 512]
                    src = pp[:]
                if g == 0:
                    T.op("dve", lambda e, dst=dst, src=src: e.tensor_copy(out=dst, in_=src), r=[pk], w=[ak])
                else:
                    T.op("dve", lambda e, dst=dst, src=src: e.tensor_tensor(out=dst, in0=dst, in1=src, op=ALU.add),
                         r=[pk, ak], w=[ak])
        if g == 2 and p == 7:
            nbuf = (h * 4 + nb) % 2
            T.op("dve", lambda e: e.reciprocal(out=den[nbuf][:], in_=den[nbuf][:]), r=[("den", nbuf)], w=[("den", nbuf)])
            T.op("pool", lambda e: e.tensor_tensor(out=ost[nbuf][:], in0=num[nbuf][:], in1=den[nbuf][:], op=ALU.mult),
                 r=[("num", nbuf), ("den", nbuf)], w=[("ost", nbuf)])
            dma(T, C.oT[h * 128:(h + 1) * 128, nb * BT:(nb + 1) * BT], ost[nbuf][:], [("ost", nbuf)], [("dr", "oT")],
                ("ost", nbuf))

    npairs = len(pairs)
    LA = 3
    load(0)
    for pi in range(min(LA, npairs)):
        emit_s(pi)
    for pi in range(npairs):
        emit_rest(pi)
        si, p = pairs[pi]
        if p == 0 and si + 1 < len(steps):
            load(si + 1)
        if pi + LA < npairs:
            emit_s(pi + LA)
    T.fence([("dr", "oT")])


def phase_ffn(T, C, layer, xsrc, xdst, wo_b, xdst_key):
    TS = 1024
    vec, cst, ones_f, ones_b = load_consts(T, C)
    oTsb = [T.sb(f"oTs{i}", [128, 8, TS], BF16) for i in range(2)]
    xs = T.sb("xs", [128, 8, TS], F32)
    h2 = T.sb("h2", [128, 8, TS], BF16)
    aT = T.sb("aT", [128, 16, TS], BF16)
    sqx = [T.sb(f"sqx{i}", [128, TS], BF16) for i in range(2)]
    rr = T.sb("rr", [128, TS], F32)
    rl = [T.sb(f"rl{i}", [128, 512], F32) for i in range(2)]
    tmp = [T.sb(f"tmp{i}", [128, 512], F32) for i in range(2)]
    wb = [T.sb(f"wb{i}", [128, 8, 512], BF16) for i in range(3)]
    w2b = [T.sb(f"w2b{i}", [128, 16, 256], BF16) for i in range(2)]
    pA = [T.ps(f"pA{i}") for i in range(4)]
    pX = [T.ps(f"pX{i}") for i in range(2)]
    wo_v = wo_b.rearrange("(k p) n -> p k n", p=128)
    w1_v = C.w1_b[layer].rearrange("(k p) n -> p k n", p=128)
    w2_v = C.w2_b[layer].rearrange("(f p) n -> p f n", p=128)
    xsrc_v = xsrc.rearrange("(k p) t -> p k t", p=128)
    xdst_v = xdst.rearrange("(k p) t -> p k t", p=128)
    oT_v = C.oT.rearrange("(k p) t -> p k t", p=128)
    gcol0 = V_NFFN[layer]
    cnt = [0, 0, 0]

    def next_pa():
        i = cnt[0] % 4
        cnt[0] += 1
        return pA[i], ("pA", i)

    NSB = S // TS

    def load_oTs(sb_):
        i_ = sb_ % 2
        dma(T, oTsb[i_][:], oT_v[:, :, sb_ * TS:(sb_ + 1) * TS], [("dr", "oT")], [("oTs", i_)], ("oTs", i_))

    load_oTs(0)
    for sbi in range(NSB):
        t0 = sbi * TS
        oTs = oTsb[sbi % 2]
        oTk = ("oTs", sbi % 2)
        for kc in range(8):
            dma(T, xs[:, kc, :], xsrc_v[:, kc, t0:t0 + TS], [("dr", "xsrc")], [("xs", kc)], ("xsl", kc))
        if sbi + 1 < NSB:
            load_oTs(sbi + 1)
        for og in range(2):
            wi = cnt[1] % 3
            cnt[1] += 1
            dma(T, wb[wi][:], wo_v[:, :, og * 512:(og + 1) * 512], [("dr", "w")], [("wb", wi)], ("wb", wi))
            for ol in range(4):
                oc = og * 4 + ol
                for tb in range(2):
                    sl = slice(tb * 512, (tb + 1) * 512)
                    pa, pk = next_pa()
                    for kc in range(8):
                        mm(T, pa[:], wb[wi][:, kc, ol * 128:(ol + 1) * 128], oTs[:, kc, sl], kc == 0, kc == 7,
                           [("wb", wi), oTk], [pk])
                    T.op("dve", lambda e, pa=pa, oc=oc, sl=sl: e.tensor_tensor(out=xs[:, oc, sl], in0=xs[:, oc, sl],
                                                                                in1=pa[:], op=ALU.add),
                         r=[pk, ("xs", oc)], w=[("xs", oc)])
        for kc in range(8):
            b = kc % 2
            T.op("act", lambda e, b=b, kc=kc: e.activation(out=sqx[b][:], in_=xs[:, kc, :], func=AF.Square),
                 r=[("xs", kc)], w=[("sqx", b)])
            T.op("act", lambda e, kc=kc: e.activation(out=h2[:, kc, :], in_=xs[:, kc, :], func=AF.Copy,
                                                       scale=vec[:, gcol0 + kc:gcol0 + kc + 1]),
                 r=[("xs", kc), "vec"], w=["h2"])
            for tb in range(2):
                mm(T, pX[tb][:], ones_b[:], sqx[b][:, tb * 512:(tb + 1) * 512], kc == 0, kc == 7,
                   ["ones_b", ("sqx", b)], [("pX", tb)])
        for tb in range(2):
            sl = slice(tb * 512, (tb + 1) * 512)
            T.op("act", lambda e, tb=tb, sl=sl: e.activation(out=rr[:, sl], in_=pX[tb][:], func=AF.Ln, scale=1.0 / D, bias=EPS),
                 r=[("pX", tb)], w=["rr"])
        T.op("act", lambda e: e.activation(out=rr[:], in_=rr[:], func=AF.Exp, scale=-1.0), r=["rr"], w=["rr"])
        for hf in range(2):
            for fg in range(4):
                wi = cnt[1] % 3
                cnt[1] += 1
                c0 = hf * 2048 + fg * 512
                dma(T, wb[wi][:], w1_v[:, :, c0:c0 + 512], [("dr", "w")], [("wb", wi)], ("wb", wi))
                for fl in range(4):
                    f = fg * 4 + fl
                    for tb in range(2):
                        sl = slice(tb * 512, (tb + 1) * 512)
                        pa, pk = next_pa()
                        for kc in range(8):
                            mm(T, pa[:], wb[wi][:, kc, fl * 128:(fl + 1) * 128], h2[:, kc, sl], kc == 0, kc == 7,
                               [("wb", wi), "h2"], [pk])
                        j = cnt[0] % 2
                        T.op("act", lambda e, pa=pa, j=j: e.activation(out=rl[j][:], in_=pa[:], func=AF.Relu),
                             r=[pk], w=[("rl", j)])
                        T.op("pool", lambda e, j=j, f=f, sl=sl: e.tensor_tensor(out=aT[:, f, sl], in0=rl[j][:], in1=rl[j][:],
                                                                                op=ALU.mult),
                             r=[("rl", j)], w=["aT"])
            for o2 in range(4):
                wi = cnt[2] % 2
                cnt[2] += 1
                dma(T, w2b[wi][:], w2_v[:, hf * 16:(hf + 1) * 16, o2 * 256:(o2 + 1) * 256], [("dr", "w")], [("w2b", wi)],
                    ("w2b", wi))
                for ol in range(2):
                    oc = o2 * 2 + ol
                    for tb in range(2):
                        sl = slice(tb * 512, (tb + 1) * 512)
                        pa, pk = next_pa()
                        for f in range(16):
                            mm(T, pa[:], w2b[wi][:, f, ol * 128:(ol + 1) * 128], aT[:, f, sl], f == 0, f == 15,
                               [("w2b", wi), "aT"], [pk])
                        j = cnt[0] % 2
                        T.op("dve", lambda e, pa=pa, j=j, sl=sl: e.tensor_tensor(out=tmp[j][:], in0=pa[:], in1=rr[:, sl],
                                                                                op=ALU.mult),
                             r=[pk, "rr"], w=[("tmp", j)])
                        T.op("pool", lambda e, j=j, oc=oc, sl=sl: e.tensor_tensor(out=xs[:, oc, sl], in0=xs[:, oc, sl],
                                                                                  in1=tmp[j][:], op=ALU.add),
                             r=[("tmp", j), ("xs", oc)], w=[("xs", oc)])
                    if hf == 1:
                        dma(T, xdst_v[:, oc, t0:t0 + TS], xs[:, oc, :], [("xs", oc)], [(xdst_key, oc)], ("xss", oc), eng="act")
    T.fence([(xdst_key, oc) for oc in range(8)])


def phase_b1(T, C):
    TS = 1024
    vec, cst, ones_f, ones_b = load_consts(T, C)
    xin = [T.sb(f"xin{i}", [128, TS], F32) for i in range(2)]
    sqx = [T.sb(f"sqx{i}", [128, TS], BF16) for i in range(2)]
    hT = T.sb("hT", [128, 8, TS], BF16)
    rbc = T.sb("rbc", [128, TS], F32)
    cosb = T.sb("cosb", [64, TS], F32)
    sinb = T.sb("sinb", [64, TS], F32)
    win = T.sb("win", [128, 8, 448], BF16)
    wqb = T.sb("wqb", [128, 2, 1536], BF16)
    wkvb = T.sb("wkvb", [128, 2048], BF16)
    ucq = T.sb("ucq", [128, 2, TS], F32)
    uckv = T.sb("uckv", [128, TS], F32)
    cqn = T.sb("cqn", [128, 2, TS], BF16)
    ckvn = T.sb("ckvn", [128, TS], BF16)
    kraw = T.sb("kraw", [64, TS], F32)
    krr = T.sb("krr", [64, TS], F32)
    sqkr = T.sb("sqkr", [64, TS], BF16)
    sqb = [T.sb(f"sqb{i}", [128, 512], BF16) for i in range(2)]
    sqr = [T.sb(f"sqr{i}", [64, 512], BF16) for i in range(2)]
    gub = [T.sb(f"gub{i}", [64, 512], BF16) for i in range(2)]
    t1 = [T.sb(f"t1{i}", [64, 512], F32) for i in range(2)]
    t2 = [T.sb(f"t2{i}", [64, 512], F32) for i in range(2)]
    rs = [T.sb(f"rs{i}", [128, 512], F32) for i in range(2)]
    qno = [T.sb(f"qno{i}", [128, TS], BF16) for i in range(2)]
    qro = [T.sb(f"qro{i}", [64, TS], BF16) for i in range(2)]
    vst = [T.sb(f"vst{i}", [128, 512], BF16) for i in range(3)]
    pA = [T.ps(f"pA{i}") for i in range(2)]
    pB = [T.ps(f"pB{i}") for i in range(2)]
    pS = [T.ps(f"pS{i}") for i in range(2)]
    pR = [T.ps(f"pR{i}") for i in range(2)]
    stp = [(pS[0], ("pS", 0)), (pS[1], ("pS", 1))]
    dma(T, win[:], C.win_b.rearrange("(k p) n -> p k n", p=128), [("dr", "w")], ["win"], "win")
    dma(T, wqb[:], C.wqb_b.rearrange("(k p) n -> p k n", p=128), [("dr", "w")], ["wqb"], "wqb")
    dma(T, wkvb[:], C.wkvb_b, [("dr", "w")], ["wkvb"], "wkvb")
    pmb = cst[0:64, CB_PMB:CB_PMB + 64]
    cnt = [0, 0, 0]

    def nxt():
        j = cnt[0] % 2
        cnt[0] += 1
        return j

    for sbi in range(S // TS):
        t0 = sbi * TS
        norm_prologue(T, C, C.xmid, t0, TS, V_NMIX[1], xin, sqx, hT, rbc, stp, vec, ones_b)
        dma(T, cosb[:], C.cosB[:, t0:t0 + TS], [], ["cosb"], "cosb")
        dma(T, sinb[:], C.sinB[:, t0:t0 + TS], [], ["sinb"], "sinb")
        for tb in range(2):
            sl = slice(tb * 512, (tb + 1) * 512)
            for grp, chunks, nrm, gc0 in (("cq", (0, 1), 256.0, V_QA), ("ckv", (2,), 128.0, V_KVA)):
                js = nxt()
                for ci, oc in enumerate(chunks):
                    j = nxt()
                    for kc in range(8):
                        mm(T, pA[j][:], win[:, kc, oc * 128:(oc + 1) * 128], hT[:, kc, sl], kc == 0, kc == 7,
                           ["win", "hT"], [("pA", j)])
                    T.op("act", lambda e, j=j: e.activation(out=sqb[j][:], in_=pA[j][:], func=AF.Square),
                         r=[("pA", j)], w=[("sqb", j)])
                    udst = ucq[:, oc, sl] if grp == "cq" else uckv[:, sl]
                    T.op("dve", lambda e, j=j, udst=udst: e.tensor_copy(out=udst, in_=pA[j][:]), r=[("pA", j)], w=["u" + grp])
                    mm(T, pS[js][:], ones_b[:], sqb[j][:], ci == 0, ci == len(chunks) - 1, ["ones_b", ("sqb", j)],
                       [("pS", js)])
                rstd_from_ss(T, pS[js][:], ("pS", js), rs[js][:], ("rs", js), 1.0 / nrm, None, None)
                for ci, oc in enumerate(chunks):
                    usrc = ucq[:, oc, sl] if grp == "cq" else uckv[:, sl]
                    odst = cqn[:, oc, sl] if grp == "cq" else ckvn[:, sl]
                    T.op("dve", lambda e, usrc=usrc, odst=odst, ci=ci, js=js, gc0=gc0: e.scalar_tensor_tensor(
                        out=odst, in0=usrc, scalar=vec[:, gc0 + ci:gc0 + ci + 1], in1=rs[js][:], op0=ALU.mult, op1=ALU.mult),
                        r=["u" + grp, ("rs", js), "vec"], w=[grp + "n"])
            j = nxt()
            for kc in range(8):
                mm(T, pA[j][0:64, :], win[:, kc, 384:448], hT[:, kc, sl], kc == 0, kc == 7, ["win", "hT"], [("pA", j)])
            T.op("dve", lambda e, j=j, sl=sl: e.tensor_copy(out=kraw[:, sl], in_=pA[j][0:64, :]),
                 r=[("pA", j)], w=["kraw"])
            T.op("act", lambda e, sl=sl: e.activation(out=sqkr[:, sl], in_=kraw[:, sl], func=AF.Square),
                 r=["kraw"], w=["sqkr"])
            T.op("act", lambda e, j=j, sl=sl: e.activation(out=gub[j][:], in_=kraw[:, sl], func=AF.Copy,
                                                          scale=vec[0:64, V_KGR:V_KGR + 1]),
                 r=["kraw", "vec"], w=[("gub", j)])
            mm(T, pR[j][0:64, :], pmb, gub[j][:], True, True, ["cst", ("gub", j)], [("pR", j)])
            T.op("dve", lambda e, j=j, sl=sl: e.scalar_tensor_tensor(out=t1[j][:], in0=kraw[:, sl],
                                                                      scalar=vec[0:64, V_KGR:V_KGR + 1], in1=cosb[:, sl],
                                                                      op0=ALU.mult, op1=ALU.mult),
                 r=["kraw", "vec", "cosb"], w=[("t1", j)])
            T.op("dve", lambda e, j=j, sl=sl: e.tensor_tensor(out=t2[j][:], in0=pR[j][0:64, :], in1=sinb[:, sl], op=ALU.mult),
                 r=[("pR", j), "sinb"], w=[("t2", j)])
            T.op("pool", lambda e, j=j, sl=sl: e.tensor_tensor(out=krr[:, sl], in0=t1[j][:], in1=t2[j][:], op=ALU.add),
                 r=[("t1", j), ("t2", j)], w=["krr"])
        for h in range(8):
            oi = h % 2
            for tb in range(2):
                sl = slice(tb * 512, (tb + 1) * 512)
                j = nxt()
                for kc in range(2):
                    mm(T, pA[j][:], wqb[:, kc, h * 192:h * 192 + 128], cqn[:, kc, sl], kc == 0, kc == 1,
                       ["wqb", "cqn"], [("pA", j)])
                for kc in range(2):
                    mm(T, pB[j][0:64, :], wqb[:, kc, h * 192 + 128:h * 192 + 192], cqn[:, kc, sl], kc == 0, kc == 1,
                       ["wqb", "cqn"], [("pB", j)])
                T.op("act", lambda e, j=j: e.activation(out=sqb[j][:], in_=pA[j][:], func=AF.Square),
                     r=[("pA", j)], w=[("sqb", j)])
                T.op("act", lambda e, j=j: e.activation(out=sqr[j][:], in_=pB[j][0:64, :], func=AF.Square),
                     r=[("pB", j)], w=[("sqr", j)])
                mm(T, pS[j][:], ones_b[:], sqb[j][:], True, False, ["ones_b", ("sqb", j)], [("pS", j)])
                mm(T, pS[j][:], ones_b[0:64, :], sqr[j][:], False, True, ["ones_b", ("sqr", j)], [("pS", j)])
                rstd_from_ss(T, pS[j][:], ("pS", j), rs[j][:], ("rs", j), 1.0 / 192, None, None)
                T.op("dve", lambda e, j=j, sl=sl, oi=oi: e.scalar_tensor_tensor(
                    out=qno[oi][:, sl], in0=pA[j][:], scalar=vec[:, V_QG:V_QG + 1], in1=rs[j][:], op0=ALU.mult, op1=ALU.mult),
                    r=[("pA", j), ("rs", j), "vec"], w=[("qno", oi)])
                T.op("act", lambda e, j=j: e.activation(out=gub[j][:], in_=pB[j][0:64, :], func=AF.Copy,
                                                        scale=vec[0:64, V_QGR:V_QGR + 1]),
                     r=[("pB", j), "vec"], w=[("gub", j)])
                mm(T, pR[j][0:64, :], pmb, gub[j][:], True, True, ["cst", ("gub", j)], [("pR", j)])
                T.op("dve", lambda e, j=j, sl=sl: e.scalar_tensor_tensor(out=t1[j][:], in0=pB[j][0:64, :],
                                                                          scalar=vec[0:64, V_QGR:V_QGR + 1], in1=cosb[:, sl],
                                                                          op0=ALU.mult, op1=ALU.mult),
                     r=[("pB", j), "vec", "cosb"], w=[("t1", j)])
                T.op("dve", lambda e, j=j, sl=sl: e.tensor_tensor(out=t2[j][:], in0=pR[j][0:64, :], in1=sinb[:, sl],
                                                                   op=ALU.mult),
                     r=[("pR", j), "sinb"], w=[("t2", j)])
                T.op("pool", lambda e, j=j: e.tensor_tensor(out=t1[j][:], in0=t1[j][:], in1=t2[j][:], op=ALU.add),
                     r=[("t1", j), ("t2", j)], w=[("t1", j)])
                T.op("pool", lambda e, j=j, sl=sl, oi=oi: e.tensor_tensor(out=qro[oi][:, sl], in0=t1[j][:], in1=rs[j][0:64, :],
                                                                          op=ALU.mult),
                     r=[("t1", j), ("rs", j)], w=[("qro", oi)])
            dma(T, C.qn[h, :, t0:t0 + TS], qno[oi][:], [("qno", oi)], [("dr", "qn")], ("qno", oi))
            dma(T, C.qr[h, :, t0:t0 + TS], qro[oi][:], [("qro", oi)], [("dr", "qr")], ("qro", oi))
        for h in range(8):
            oi = h % 2
            for tb in range(2):
                sl = slice(tb * 512, (tb + 1) * 512)
                j = nxt()
                mm(T, pA[j][:], wkvb[:, h * 256:h * 256 + 128], ckvn[:, sl], True, True, ["wkvb", "ckvn"], [("pA", j)])
                T.op("act", lambda e, j=j: e.activation(out=sqb[j][:], in_=pA[j][:], func=AF.Square),
                     r=[("pA", j)], w=[("sqb", j)])
                mm(T, pS[j][:], ones_b[:], sqb[j][:], True, False, ["ones_b", ("sqb", j)], [("pS", j)])
                mm(T, pS[j][:], ones_b[0:64, :], sqkr[:, sl], False, True, ["ones_b", "sqkr"], [("pS", j)])
                rstd_from_ss(T, pS[j][:], ("pS", j), rs[j][:], ("rs", j), 1.0 / 192, None, None)
                T.op("dve", lambda e, j=j, sl=sl, oi=oi: e.scalar_tensor_tensor(
                    out=qno[oi][:, sl], in0=pA[j][:], scalar=vec[:, V_KG:V_KG + 1], in1=rs[j][:], op0=ALU.mult, op1=ALU.mult),
                    r=[("pA", j), ("rs", j), "vec"], w=[("qno", oi)])
                T.op("pool", lambda e, j=j, sl=sl, oi=oi: e.tensor_tensor(out=qro[oi][:, sl], in0=krr[:, sl], in1=rs[j][0:64, :],
                                                                          op=ALU.mult),
                     r=["krr", ("rs", j)], w=[("qro", oi)])
            dma(T, C.kn[h, :, t0:t0 + TS], qno[oi][:], [("qno", oi)], [("dr", "kn")], ("qno", oi))
            dma(T, C.kr[h, :, t0:t0 + TS], qro[oi][:], [("qro", oi)], [("dr", "kr")], ("qro", oi))
        wv = wkvb[:].rearrange("p (h x) -> p h x", x=256)
        for tt in range(TS // 128):
            for cbk in range(2):
                j = nxt()
                vi = cnt[2] % 3
                cnt[2] += 1
                mm(T, pA[j][:].rearrange("p (h x) -> p h x", x=128), ckvn[:, tt * 128:(tt + 1) * 128],
                   wv[:, cbk * 4:(cbk + 1) * 4, 128:256], True, True,
                   ["ckvn", "wkvb"], [("pA", j)])
                T.op("act", lambda e, j=j, vi=vi: e.activation(out=vst[vi][:], in_=pA[j][:], func=AF.Copy),
                     r=[("pA", j)], w=[("vst", vi)])
                dma(T, C.vbm[t0 + tt * 128:t0 + (tt + 1) * 128, cbk * 512:(cbk + 1) * 512], vst[vi][:], [("vst", vi)],
                    [("dr", "vbm")], ("vst", vi))
    T.fence([("dr", "qn"), ("dr", "qr"), ("dr", "kn"), ("dr", "kr"), ("dr", "vbm")])


def phase_b2(T, C):
    vec, cst, ones_f, ones_b = load_consts(T, C)
    knb = [T.sb(f"knb{i}", [128, S], BF16) for i in range(2)]
    krb = [T.sb(f"krb{i}", [128, S // 2], BF16) for i in range(2)]
    vb = [T.sb(f"vb{i}", [128, 64, 128], BF16) for i in range(2)]
    qnb = [T.sb(f"qnb{i}", [128, 512], BF16) for i in range(3)]
    qrb = [T.sb(f"qrb{i}", [128, 512], BF16) for i in range(3)]
    pT = [T.sb(f"pT{i}", [128, 512], BF16) for i in range(4)]
    rden = [T.sb(f"rden{i}", [128, 512], F32) for i in range(2)]
    spb = [T.sb(f"spb{i}", [128, 512], BF16) for i in range(3)]
    ost = [T.sb(f"ost{i}", [128, 2048], BF16) for i in range(2)]
    pS = [T.ps(f"pS{i}") for i in range(4)]
    pO = [T.ps(f"pO{i}") for i in range(2)]
    pD = [T.ps(f"pD{i}") for i in range(2)]
    scale = 1.0 / np.sqrt(192.0)
    NQ = S // 512
    NK = S // 128

    def load_head(h):
        i = h % 2
        dma(T, knb[i][:], C.kn[h], [("dr", "kn")], [("knb", i)], ("knb", i))
        krsrc = C.kr[h].rearrange("p (j two c) -> p j two c", two=2, c=128)
        for half in range(2):
            dma(T, krb[i][half * 64:(half + 1) * 64, :].rearrange("p (j c) -> p j c", c=128), krsrc[:, :, half, :],
                [("dr", "kr")], [("krb", i)], ("krb", i))
        for q4 in range(4):
            src = C.vbm[q4 * 2048:(q4 + 1) * 2048, h * 128:(h + 1) * 128].rearrange("(j i) d -> i j d", i=128)
            dma(T, vb[i][:, q4 * 16:(q4 + 1) * 16, :], src, [("dr", "vbm")], [("vb", i)], ("vb", i))

    qlist = [(h, qb_) for h in range(8) for qb_ in range(NQ)]

    def load_q(qi):
        h, qb_ = qlist[qi]
        i = qi % 3
        dma(T, qnb[i][:], C.qn[h, :, qb_ * 512:(qb_ + 1) * 512], [("dr", "qn")], [("qnb", i)], ("qnb", i))
        for half in range(2):
            dma(T, qrb[i][half * 64:(half + 1) * 64, :], C.qr[h, :, qb_ * 512:(qb_ + 1) * 512], [("dr", "qr")],
                [("qrb", i)], ("qrb", i))

    tiles = [(qi, kt) for qi in range(len(qlist)) for kt in range(NK)]
    hl = [-1]
    ql = [-1]

    def emit_qk(tp):
        ti0 = 2 * tp
        qi, kt0 = tiles[ti0]
        h, qb_ = qlist[qi]
        while hl[0] < min(h + 1, 7) and (hl[0] < h or (qb_ >= NQ - 2)):
            hl[0] += 1
            load_head(hl[0])
        while ql[0] < min(qi + 1, len(qlist) - 1):
            ql[0] += 1
            load_q(ql[0])
        i = h % 2
        q3 = qi % 3
        for s_ in range(2):
            ti = ti0 + s_
            kt = kt0 + s_
            mm(T, pS[ti % 4][:], knb[i][:, kt * 128:(kt + 1) * 128], qnb[q3][:], True, False, [("knb", i), ("qnb", q3)],
               [("pS", ti % 4)])
        j = kt0 // 2
        for s_ in range(2):
            ti = ti0 + s_
            mm(T, pS[ti % 4][:], krb[i][s_ * 64:(s_ + 1) * 64, j * 128:(j + 1) * 128], qrb[q3][s_ * 64:(s_ + 1) * 64, :],
               False, True, [("krb", i), ("qrb", q3)], [("pS", ti % 4)])

    def emit_pv(ti):
        qi, kt = tiles[ti]
        h, qb_ = qlist[qi]
        i = h % 2
        ps = pS[ti % 4]
        pt = pT[ti % 4]
        ob = qi % 2
        T.op("act", lambda e: e.activation(out=pt[:], in_=ps[:], func=AF.Exp, scale=float(scale)),
             r=[("pS", ti % 4)], w=[("pT", ti % 4)])
        mm(T, pO[ob][:], vb[i][:, kt, :], pt[:], kt == 0, kt == NK - 1, [("vb", i), ("pT", ti % 4)], [("pO", ob)])

    def emit_den(tp):
        qi, kt0 = tiles[2 * tp]
        h, qb_ = qlist[qi]
        ob = qi % 2
        x = tp % 3
        mm(T, pD[ob][:], ones_b[:], spb[x][:], kt0 == 0, kt0 == NK - 2, ["ones_b", ("spb", x)], [("pD", ob)])
        if kt0 == NK - 2:
            oi = (qi // 4) % 2
            q4 = qi % 4
            T.op("dve", lambda e: e.reciprocal(out=rden[ob][:], in_=pD[ob][:]), r=[("pD", ob)], w=[("rden", ob)])
            T.op("dve", lambda e: e.tensor_tensor(out=ost[oi][:, q4 * 512:(q4 + 1) * 512], in0=pO[ob][:], in1=rden[ob][:],
                                                  op=ALU.mult),
                 r=[("pO", ob), ("rden", ob)], w=[("ost", oi)])
            if q4 == 3:
                b4 = qb_ // 4
                dma(T, C.oT[h * 128:(h + 1) * 128, b4 * 2048:(b4 + 1) * 2048], ost[oi][:], [("ost", oi)], [("dr", "oT")],
                    ("ost", oi))

    n = len(tiles)
    npair = n // 2
    emit_qk(0)
    for tp in range(npair):
        if tp + 1 < npair:
            emit_qk(tp + 1)
        emit_pv(2 * tp)
        emit_pv(2 * tp + 1)
        x = tp % 3
        t0_, t1_ = (2 * tp) % 4, (2 * tp + 1) % 4
        T.op("dve", lambda e, x=x, t0_=t0_, t1_=t1_: e.tensor_tensor(out=spb[x][:], in0=pT[t0_][:], in1=pT[t1_][:], op=ALU.add),
             r=[("pT", t0_), ("pT", t1_)], w=[("spb", x)])
        if tp >= 1:
            emit_den(tp - 1)
    emit_den(npair - 1)
    T.fence([("dr", "oT")])


def build_program(last_phase=7, debug=False, dbg_lvl=0):
    nc = bass.Bass("TRN2", target_bir_lowering=False)
    C = Ctx()
    C.dbg_lvl = dbg_lvl
    C.nc = nc

    def dr(name, shape, dt, kind="Internal"):
        return nc.dram_tensor(name, shape, dt, kind=kind).ap()

    ein = "ExternalInput"
    C.xT = dr("xT", [D, S], F32, ein)
    C.vecs = dr("vecs", [128, NV], F32, ein)
    C.cb16 = dr("cb16", [128, NCB], BF16, ein)
    C.cosA = dr("cosA", [128, S], F32, ein)
    C.sinA = dr("sinA", [128, S], F32, ein)
    C.cosB = dr("cosB", [64, S], F32, ein)
    C.sinB = dr("sinB", [64, S], F32, ein)
    C.a_w_qkv = dr("a_w_qkv", [D, 9216], F32, ein)
    a_w_o = dr("a_w_o", [D, D], F32, ein)
    b_w_in = dr("b_w_in", [D, 448], F32, ein)
    b_w_qb = dr("b_w_qb", [256, 1536], F32, ein)
    b_w_kvb = dr("b_w_kvb", [128, 2048], F32, ein)
    b_w_o = dr("b_w_o", [D, D], F32, ein)
    ffn_w1 = dr("ffn_w1", [2, D, 4096], F32, ein)
    ffn_w2 = dr("ffn_w2", [2, 4096, D], F32, ein)
    C.yT = dr("yT", [D, S], F32, "ExternalOutput")
    dbg_kind = "ExternalOutput" if debug else "Internal"
    C.xmid = dr("xmid", [D, S], F32, dbg_kind)
    C.oT = dr("oT", [D, S], BF16, dbg_kind)
    C.wqkv_b = dr("wqkv_b", [D, 9216], BF16)
    C.woa_b = dr("woa_b", [D, D], BF16)
    C.wob_b = dr("wob_b", [D, D], BF16)
    C.win_b = dr("win_b", [D, 448], BF16)
    C.wqb_b = dr("wqb_b", [256, 1536], BF16)
    C.wkvb_b = dr("wkvb_b", [128, 2048], BF16)
    C.w1_b = dr("w1_b", [2, D, 4096], BF16)
    C.w2_b = dr("w2_b", [2, 4096, D], BF16)
    C.qk_a = dr("qk_a", [2, 3, 8, 128, S], BF16)
    C.va = dr("va", [3, S, D], BF16)
    C.qn = dr("qn", [8, 128, S], BF16)
    C.qr = dr("qr", [8, 64, S], BF16)
    C.kn = dr("kn", [8, 128, S], BF16)
    C.kr = dr("kr", [8, 64, S], BF16)
    C.vbm = dr("vbm", [S, D], BF16)
    C.casts = [
        (C.woa_b, a_w_o, 1024, 512),
        (C.w1_b[0], ffn_w1[0], 1024, 256),
        (C.w2_b[0], ffn_w2[0], 4096, 1024),
        (C.win_b, b_w_in, 1024, 1024),
        (C.wqb_b, b_w_qb, 256, 256),
        (C.wkvb_b, b_w_kvb, 128, 128),
        (C.wob_b, b_w_o, 1024, 512),
        (C.w1_b[1], ffn_w1[1], 1024, 256),
        (C.w2_b[1], ffn_w2[1], 4096, 1024),
    ]
    phases = [
        lambda T: phase_cast(T, C),
        lambda T: phase_a1(T, C),
        lambda T: phase_a2(T, C),
        lambda T: phase_ffn(T, C, 0, C.xT, C.xmid, C.woa_b, ("dr", "xmid")),
        lambda T: phase_b1(T, C),
        lambda T: phase_b2(T, C),
        lambda T: phase_ffn(T, C, 1, C.xmid, C.yT, C.wob_b, ("dr", "yT")),
    ]
    for pi, ph in enumerate(phases):
        if pi >= last_phase:
            break
        run_phase(nc, ph, f"p{pi}_")
    return nc


def _rope_tables(d):
    pos = np.arange(S, dtype=np.float32)
    freqs = (np.float32(10000.0) ** (-np.arange(0, d, 2, dtype=np.float32) / np.float32(d))).astype(np.float32)
    ang = (pos[:, None] * freqs[None, :]).astype(np.float32)
    cos = np.concatenate([np.cos(ang), np.cos(ang)], -1).astype(np.float32)
    sin = np.concatenate([np.sin(ang), np.sin(ang)], -1).astype(np.float32)
    return np.ascontiguousarray(cos.T), np.ascontiguousarray(sin.T)


def _consts():
    cb = np.zeros((128, NCB), np.float32)
    for i in range(64):
        cb[i + 64, CB_PMA + i] = -1.0
        cb[i, CB_PMA + i + 64] = 1.0
    for i in range(32):
        cb[i + 32, CB_PMB + i] = -1.0
        cb[i, CB_PMB + i + 32] = 1.0
    kk = np.arange(128)[:, None]
    qq = np.arange(128)[None, :]
    mA = (qq <= kk).astype(np.float32)
    mB = (qq >= kk).astype(np.float32)
    normal = np.concatenate([mA, mB], 1)
    first = np.concatenate([mA * (kk >= 64), mB], 1)
    last = np.concatenate([mA, mB * (kk < 64)], 1)
    cb[:, CB_MK:CB_MK + 256] = normal
    cb[:, CB_MK + 256:CB_MK + 512] = first
    cb[:, CB_MK + 512:CB_MK + 768] = last
    cb[:, CB_MKN2:CB_MKN2 + 256] = normal
    cb[:, CB_MKN2 + 256:CB_MKN2 + 512] = normal
    return cb.astype(ml_dtypes.bfloat16)


def _pack_vecs(inp):
    v = np.zeros((128, NV), np.float32)
    for l in range(2):
        v[:, V_NMIX[l]:V_NMIX[l] + 8] = inp["norm_mix"][l].reshape(8, 128).T
        v[:, V_NFFN[l]:V_NFFN[l] + 8] = inp["norm_ffn"][l].reshape(8, 128).T
    for g in range(3):
        v[:, V_AQ + g] = inp["a_q_gain"][0, g]
        v[:, V_AK + g] = inp["a_k_gain"][0, g]
    v[:, V_QA:V_QA + 2] = inp["b_q_a_gain"][0].reshape(2, 128).T
    v[:, V_KVA] = inp["b_kv_a_gain"][0]
    v[:, V_QG] = inp["b_q_gain"][0, :128]
    v[:64, V_QGR] = inp["b_q_gain"][0, 128:]
    v[:, V_KG] = inp["b_k_gain"][0, :128]
    v[:64, V_KGR] = inp["b_k_gain"][0, 128:]
    v[0, V_E0] = 1.0
    return v


def make_in_maps(inp, cores):
    cosA, sinA = _rope_tables(128)
    cosB, sinB = _rope_tables(64)
    shared = {
        "vecs": _pack_vecs(inp), "cb16": _consts(), "cosA": cosA, "sinA": sinA, "cosB": cosB, "sinB": sinB,
        "a_w_qkv": np.ascontiguousarray(inp["a_w_qkv"][0]), "a_w_o": np.ascontiguousarray(inp["a_w_o"][0]),
        "b_w_in": np.ascontiguousarray(inp["b_w_in"][0]), "b_w_qb": np.ascontiguousarray(inp["b_w_qb"][0]),
        "b_w_kvb": np.ascontiguousarray(inp["b_w_kvb"][0]), "b_w_o": np.ascontiguousarray(inp["b_w_o"][0]),
        "ffn_w1": np.ascontiguousarray(inp["ffn_w1"]), "ffn_w2": np.ascontiguousarray(inp["ffn_w2"]),
    }
    maps = []
    for b in cores:
        m = dict(shared)
        m["xT"] = np.ascontiguousarray(np.asarray(inp["x"][b]).T)
        maps.append(m)
    return maps


def kernel(**inputs):
    inp = {k: np.asarray(v) for k, v in inputs.items()}
    nc = build_program()
    maps = make_in_maps(inp, list(range(NCORES)))
    res = run_bass_kernel_spmd(nc, maps, core_ids=list(range(NCORES)))
    out = np.empty((NCORES, S, D), np.float32)
    for b in range(NCORES):
        out[b] = np.asarray(res.results[b]["yT"]).T
    return out
```

```python
import contextlib
import numpy as np
import ml_dtypes
import concourse.bass as bass
import concourse.mybir as mybir
from concourse.bass_utils import run_bass_kernel_spmd

F32 = mybir.dt.float32
BF16 = mybir.dt.bfloat16
AF = mybir.ActivationFunctionType
ALU = mybir.AluOpType

S = 8192
D = 1024
EPS = 1e-6
DIL = (1, 4, 16)
NCORES = 8
PSUM_NAMES = frozenset(("pA", "pB", "pS", "pR", "pX", "pO", "pD"))
import os as _os0
POOL2DVE = _os0.environ.get('K_POOL2DVE') == '1'

V_NMIX = (0, 16)
V_NFFN = (8, 24)
V_AQ = 32
V_AK = 35
V_QA = 38
V_KVA = 40
V_QG = 41
V_QGR = 42
V_KG = 43
V_KGR = 44
V_E0 = 47
NV = 48
CB_PMA = 0
CB_PMB = 128
CB_MK = 192
CB_MKN2 = 960
NCB = 1472


class _Op:
    __slots__ = ("eng", "idx", "fn", "deps", "is_dma", "dma_key", "signal", "sem", "val", "waits")

    def __init__(self, eng, idx, fn, is_dma, dma_key):
        self.eng = eng
        self.idx = idx
        self.fn = fn
        self.deps = []
        self.is_dma = is_dma
        self.dma_key = dma_key
        self.signal = False
        self.sem = None
        self.val = None
        self.waits = []


class Tk:
    ENGS = ("pe", "act", "dve", "pool", "sp")
    EPOCH = 4000

    def __init__(self, nc):
        self.nc = nc
        self.ops = {e: [] for e in self.ENGS}
        self.last_w = {}
        self.readers = {}
        self.nsem = 0

    def _newsem(self, name):
        self.nsem += 1
        return self.nc.alloc_semaphore(name=f"{self.pfx}s{self.nsem}_{name}")

    def op(self, eng, fn, r=(), w=(), dma_key=None):
        is_dma = dma_key is not None
        if eng == "pool" and not is_dma and POOL2DVE:
            eng = "dve"
        o = _Op(eng, len(self.ops[eng]), fn, is_dma, dma_key)
        deps = []
        for k in r:
            lw = self.last_w.get(k)
            if lw is not None:
                deps.append((lw, 0))
            if isinstance(k, tuple) and k[0] in PSUM_NAMES:
                for rd in self.readers.get(k, ()):
                    if rd.eng != eng:
                        deps.append((rd, 1))
        for k in w:
            lw = self.last_w.get(k)
            if lw is not None:
                deps.append((lw, 1))
            for rd in self.readers.get(k, ()):
                deps.append((rd, 1))
        for k in w:
            self.last_w[k] = o
            self.readers[k] = []
        for k in r:
            self.readers.setdefault(k, []).append(o)
        best = {}
        for d, kind in deps:
            if d is o:
                continue
            if d.is_dma:
                best[("dma", id(d))] = d
                continue
            if d.eng == eng and not is_dma:
                if kind != 0 or eng == "pe" or (o.idx - d.idx) > 2:
                    continue
            cur = best.get(d.eng)
            if cur is None or d.idx > cur.idx:
                best[d.eng] = d
        o.deps = list(best.values())
        for d in o.deps:
            d.signal = True
        self.ops[eng].append(o)
        return o

    def fence(self, keys):
        o = self.op("sp", None, r=list(keys))
        last = {}
        for eng in self.ENGS:
            for d in self.ops[eng]:
                if d.is_dma:
                    last[d.dma_key] = d
        have = set(id(d) for d in o.deps)
        for d in last.values():
            if id(d) not in have:
                o.deps.append(d)
                d.signal = True

    def finalize(self):
        dma_sems = {}
        dma_cnt = {}
        for eng in self.ENGS:
            cur_sem = None
            cnt = 0
            for o in self.ops[eng]:
                if o.is_dma:
                    k = o.dma_key
                    if k not in dma_sems:
                        dma_sems[k] = self._newsem("d")
                        dma_cnt[k] = 0
                    dma_cnt[k] += 16
                    o.sem = dma_sems[k]
                    o.val = dma_cnt[k]
                    o.signal = True
                elif o.signal:
                    if cur_sem is None or cnt >= self.EPOCH:
                        cur_sem = self._newsem(eng)
                        cnt = 0
                    cnt += 1
                    o.sem = cur_sem
                    o.val = cnt
        for eng in self.ENGS:
            waited = {}
            for o in self.ops[eng]:
                ws = {}
                for d in o.deps:
                    sid = id(d.sem)
                    if waited.get(sid, -1) >= d.val:
                        continue
                    if sid not in ws or ws[sid][1] < d.val:
                        ws[sid] = (d.sem, d.val)
                for sid, (s, v) in ws.items():
                    waited[sid] = v
                o.waits = list(ws.values())

    def emit(self, block):
        self.finalize()
        handles = {"pe": block.tensor, "act": block.scalar, "dve": block.vector,
                   "pool": block.gpsimd, "sp": block.sync}
        for eng in self.ENGS:
            ops = self.ops[eng]
            if not ops:
                continue

            def body(e, ops=ops):
                for o in ops:
                    for (s, v) in o.waits:
                        e.wait_ge(s, v)
                    ins = o.fn(e) if o.fn is not None else None
                    if o.signal:
                        ins.then_inc(o.sem, 16 if o.is_dma else 1)

            handles[eng](body)


class Ctx:
    pass


def run_phase(nc, body, pfx):
    with nc.cleanup_on_exit():
        with contextlib.ExitStack() as st:
            T = Tk(nc)
            T.pfx = pfx
            T.sb = lambda name, shape, dt: st.enter_context(nc.sbuf_tensor(pfx + name, shape, dt))
            T.ps = lambda name: st.enter_context(nc.psum_tensor(pfx + name, [128, 512], F32))
            T.ps2 = lambda name: st.enter_context(nc.psum_tensor(pfx + name, [128, 1024], F32))
            body(T)
            with nc.Block() as block:
                T.emit(block)
        nc.all_engine_barrier()


def mm(T, out, lhsT, rhs, start, stop, r, w):
    T.op("pe", lambda e: e.matmul(out, lhsT=lhsT, rhs=rhs, start=start, stop=stop), r=r, w=w)


def dma(T, out, in_, r, w, key, eng="sp"):
    T.op(eng, lambda e: e.dma_start(out=out, in_=in_), r=r, w=w, dma_key=key)


def load_consts(T, C):
    vec = T.sb("vec", [128, NV], F32)
    cst = T.sb("cst", [128, NCB], BF16)
    ones_b = T.sb("ones_b", [128, 128], BF16)
    dma(T, vec[:], C.vecs, [], ["vec"], "vec")
    dma(T, cst[:], C.cb16, [], ["cst"], "cst")
    T.op("dve", lambda e: e.memset(ones_b[:], 1.0), w=["ones_b"])
    return vec, cst, None, ones_b


def norm_prologue(T, C, xsrc, t0, TS, gcol0, xin, sqx, hT, rbc, stp, vec, ones_b):
    ntb = TS // 512
    for kc in range(8):
        b = kc % 2
        dma(T, xin[b][:], xsrc[kc * 128:(kc + 1) * 128, t0:t0 + TS], [("dr", "xsrc")], [("xin", b)], ("xin", b))
        T.op("act", lambda e, b=b: e.activation(out=sqx[b][:], in_=xin[b][:], func=AF.Square),
             r=[("xin", b)], w=[("sqx", b)])
        for tb in range(ntb):
            mm(T, stp[tb][0][:], ones_b[:], sqx[b][:, tb * 512:(tb + 1) * 512], kc == 0, kc == 7,
               ["ones_b", ("sqx", b)], [stp[tb][1]])
    for tb in range(ntb):
        sl = slice(tb * 512, (tb + 1) * 512)
        T.op("act", lambda e, tb=tb, sl=sl: e.activation(out=rbc[:, sl], in_=stp[tb][0][:], func=AF.Ln,
                                                         scale=1.0 / D, bias=EPS),
             r=[stp[tb][1]], w=["rbc"])
    T.op("act", lambda e: e.activation(out=rbc[:], in_=rbc[:], func=AF.Exp, scale=-0.5), r=["rbc"], w=["rbc"])
    for kc in range(8):
        b = kc % 2
        dma(T, xin[b][:], xsrc[kc * 128:(kc + 1) * 128, t0:t0 + TS], [("dr", "xsrc")], [("xin", b)], ("xin", b))
        T.op("dve", lambda e, b=b, kc=kc: e.scalar_tensor_tensor(out=hT[:, kc, :], in0=xin[b][:],
                                                              scalar=vec[:, gcol0 + kc:gcol0 + kc + 1], in1=rbc[:],
                                                              op0=ALU.mult, op1=ALU.mult),
             r=[("xin", b), "vec", "rbc"], w=["hT"])


def rstd_from_ss(T, pss, psk, rs, rsk, scale, bias_ap, bias_key):
    T.op("act", lambda e: e.activation(out=rs, in_=pss, func=AF.Ln, scale=scale, bias=EPS), r=[psk], w=[rsk])
    T.op("act", lambda e: e.activation(out=rs, in_=rs, func=AF.Exp, scale=-0.5), r=[rsk], w=[rsk])


def phase_cast(T, C):
    keys = []
    for i in range(8):
        dma(T, C.wqkv_b[i * 128:(i + 1) * 128, :], C.a_w_qkv[i * 128:(i + 1) * 128, :], [], [("dr", "wqkv", i)], "cast",
            eng="pool")
        keys.append(("dr", "wqkv", i))
    T.fence(keys)


def other_casts(T, C):
    for ci_, (dst, src, rows, piece) in enumerate(C.casts):
        for i in range(0, rows, piece):
            dma(T, dst[i:i + piece, :], src[i:i + piece, :], [], [("dr", "wcast", ci_, i)], "cast", eng="pool")


def phase_a1(T, C):
    TS = 2048
    vec, cst, ones_f, ones_b = load_consts(T, C)
    wq_keys = []
    cast_keys = []
    for i in range(0):
        dma(T, C.wqkv_b[i * 128:(i + 1) * 128, :], C.a_w_qkv[i * 128:(i + 1) * 128, :], [], [("dr", "wqkv", i)], "castq",
            eng="pool")
        wq_keys.append(("dr", "wqkv", i))
    lvl = getattr(C, "dbg_lvl", 0)
    if lvl == 1:
        T.fence(cast_keys + wq_keys)
        return
    xin = [T.sb(f"xin{i}", [128, TS], F32) for i in range(2)]
    sqx = [T.sb(f"sqx{i}", [128, TS], BF16) for i in range(2)]
    hT = T.sb("hT", [128, 8, TS], BF16)
    rbc = T.sb("rbc", [128, TS], F32)
    cosb = T.sb("cosb", [128, TS], F32)
    sinb = T.sb("sinb", [128, TS], F32)
    wb = [T.sb(f"wb{i}", [128, 8, 512], BF16) for i in range(3)]
    sqb = [T.sb(f"sqb{i}", [128, 512], BF16) for i in range(2)]
    gub = [T.sb(f"gub{i}", [128, 512], BF16) for i in range(2)]
    t1 = [T.sb(f"t1{i}", [128, 512], F32) for i in range(2)]
    t2 = [T.sb(f"t2{i}", [128, 512], F32) for i in range(2)]
    rs = [T.sb(f"rs{i}", [128, 512], F32) for i in range(2)]
    qst = [T.sb(f"qst{i}", [128, TS], BF16) for i in range(2)]
    vst = [T.sb(f"vst{i}", [128, 512], BF16) for i in range(3)]
    pA = [T.ps(f"pA{i}") for i in range(2)]
    pS = [T.ps(f"pS{i}") for i in range(2)]
    pR = [T.ps(f"pR{i}") for i in range(2)]
    pX = [T.ps(f"pX{i}") for i in range(2)]
    stp = [(pX[0], ("pX", 0)), (pX[1], ("pX", 1)), (pS[0], ("pS", 0)), (pS[1], ("pS", 1))]
    wq_v = C.wqkv_b.rearrange("(k p) n -> p k n", p=128)
    pma = cst[:, CB_PMA:CB_PMA + 128]
    cnt = [0, 0, 0]

    for sbi in range(S // TS):
        t0 = sbi * TS
        norm_prologue(T, C, C.xT, t0, TS, V_NMIX[0], xin, sqx, hT, rbc, stp, vec, ones_b)
        dma(T, cosb[:], C.cosA[:, t0:t0 + TS], [], ["cosb"], "cosb")
        dma(T, sinb[:], C.sinA[:, t0:t0 + TS], [], ["sinb"], "sinb")
        if lvl == 2:
            T.fence(cast_keys + wq_keys)
            return

        groups = []
        for g in range(3):
            for qk in range(2):
                for hh in range(2):
                    groups.append(("qk", g, qk, hh))
        for g in range(3):
            for cbk in range(2):
                groups.append(("v", g, 2, cbk))

        def gcol(gr):
            kind, g, qk, hh = gr
            return (qk * 3 + g) * 1024 + hh * 512

        def load_w(gi):
            i = gi % 3
            c0 = gcol(groups[gi])
            dma(T, wb[i][:], wq_v[:, :, c0:c0 + 512], wq_keys, [("wb", i)], ("wb", i))

        items = []
        for gi, gr in enumerate(groups):
            kind, g, qk, hh = gr
            r_ = DIL[g]
            if kind == "qk":
                for hl in range(4):
                    for tb in range(4):
                        items.append((gi, "qk", g, qk, hh * 4 + hl, hl, tb))
            else:
                for tile in range(16):
                    items.append((gi, "v", g, hh, tile, 0, 0))

        load_w(0)
        load_w(1)
        loaded = [1]

        def emit_proj(item, it):
            gi = item[0]
            i = gi % 3
            while loaded[0] < min(gi + 2, len(groups) - 1):
                loaded[0] += 1
                load_w(loaded[0])
            pa = pA[it % 2]
            if item[1] == "qk":
                _, _, g, qk, h, hl, tb = item
                for kc in range(8):
                    mm(T, pa[:], wb[i][:, kc, hl * 128:(hl + 1) * 128], hT[:, kc, tb * 512:(tb + 1) * 512],
                       kc == 0, kc == 7, [("wb", i), "hT"], [("pA", it % 2)])
            else:
                _, _, g, cbk, tile, _, _ = item
                r_ = DIL[g]
                tpr = 16 // r_
                c, m_ = divmod(tile, tpr)
                for kc in range(8):
                    hv = hT[:, kc, :].rearrange("p (m c) -> p c m", c=r_)
                    mm(T, pa[:], hv[:, c, m_ * 128:(m_ + 1) * 128], wb[i][:, kc, :],
                       kc == 0, kc == 7, [("wb", i), "hT"], [("pA", it % 2)])

        def emit_post(item, it):
            j = it % 2
            pa = pA[j]
            if item[1] == "qk":
                _, _, g, qk, h, hl, tb = item
                r_ = DIL[g]
                gc = (V_AQ if qk == 0 else V_AK) + g
                gap = vec[:, gc:gc + 1]
                sl = slice(tb * 512, (tb + 1) * 512)
                PLV = int(_os0.environ.get("K_POSTLV", "99"))
                T.op("act", lambda e: e.activation(out=sqb[j][:], in_=pa[:], func=AF.Square),
                     r=[("pA", j)], w=[("sqb", j)])
                if PLV < 2:
                    return
                T.op("act", lambda e: e.activation(out=gub[j][:], in_=pa[:], func=AF.Copy, scale=gap),
                     r=[("pA", j), "vec"], w=[("gub", j)])
                if PLV < 3:
                    return
                mm(T, pS[j][:], ones_b[:], sqb[j][:], True, True, ["ones_b", ("sqb", j)], [("pS", j)])
                mm(T, pR[j][:], pma, gub[j][:], True, True, ["cst", ("gub", j)], [("pR", j)])
                if PLV < 4:
                    return
                VAR = _os0.environ.get("K_VAR", "")
                if VAR == "A":
                    T.op("dve", lambda e: e.scalar_tensor_tensor(out=t1[j][:], in0=pa[:], scalar=gap, in1=rbc[:, sl],
                                                                 op0=ALU.mult, op1=ALU.mult),
                         r=[("pA", j), "vec", "rbc"], w=[("t1", j)])
                elif VAR == "B":
                    T.op("dve", lambda e: e.scalar_tensor_tensor(out=t1[j][:], in0=pa[:], scalar=1.5, in1=cosb[:, sl],
                                                                 op0=ALU.mult, op1=ALU.mult),
                         r=[("pA", j), "cosb"], w=[("t1", j)])
                elif VAR == "C":
                    T.op("dve", lambda e: e.tensor_tensor(out=t1[j][:], in0=pa[:], in1=cosb[:, sl], op=ALU.mult),
                         r=[("pA", j), "cosb"], w=[("t1", j)])
                else:
                    T.op("dve", lambda e: e.scalar_tensor_tensor(out=t1[j][:], in0=pa[:], scalar=gap, in1=cosb[:, sl],
                                                                 op0=ALU.mult, op1=ALU.mult),
                         r=[("pA", j), "vec", "cosb"], w=[("t1", j)])
                if PLV < 5:
                    return
                T.op("dve", lambda e: e.tensor_tensor(out=t2[j][:], in0=pR[j][:], in1=sinb[:, sl], op=ALU.mult),
                     r=[("pR", j), "sinb"], w=[("t2", j)])
                if PLV < 6:
                    return
                T.op("dve", lambda e: e.tensor_tensor(out=t1[j][:], in0=t1[j][:], in1=t2[j][:], op=ALU.add),
                     r=[("t1", j), ("t2", j)], w=[("t1", j)])
                if PLV < 7:
                    return
                rstd_from_ss(T, pS[j][:], ("pS", j), rs[j][:], ("rs", j), 1.0 / 128, None, None)
                if PLV < 8:
                    return
                ch = cnt[1]
                qs = qst[ch % 2]
                qv = qs[:].rearrange("p (c m) -> p c m", c=r_)
                w_ = 512 // r_
                T.op("pool", lambda e: e.tensor_tensor(out=qv[:, :, tb * w_:(tb + 1) * w_],
                                                       in0=t1[j][:].rearrange("p (m c) -> p c m", c=r_),
                                                       in1=rs[j][:].rearrange("p (m c) -> p c m", c=r_), op=ALU.mult),
                     r=[("t1", j), ("rs", j)], w=[("qst", ch % 2)])
                if tb == 3:
                    dst = C.qk_a[qk, g, h].rearrange("p (c l) -> p c l", c=r_)[:, :, sbi * (TS // r_):(sbi + 1) * (TS // r_)]
                    dma(T, dst, qv, [("qst", ch % 2)], [("dr", "qk_a")], ("qst", ch % 2))
                    cnt[1] += 1
            else:
                _, _, g, cbk, tile, _, _ = item
                r_ = DIL[g]
                L = S // r_
                tpr = 16 // r_
                c, m_ = divmod(tile, tpr)
                vi = cnt[2] % 3
                cnt[2] += 1
                col = g * 16 + tile
                T.op("act", lambda e: e.activation(out=vst[vi][:], in_=pa[:], func=AF.Copy),
                     r=[("pA", j)], w=[("vst", vi)])
                row0 = c * L + sbi * (TS // r_) + m_ * 128
                dma(T, C.va[g, row0:row0 + 128, cbk * 512:(cbk + 1) * 512], vst[vi][:], [("vst", vi)], [("dr", "va")],
                    ("vst", vi))

        base = cnt[0]
        if lvl >= 3:
            items = items[:lvl]
        n = len(items)
        import os as _os
        nopost = _os.environ.get("K_NOPOST") == "1"
        emit_proj(items[0], base)
        for ii in range(n):
            if ii + 1 < n:
                emit_proj(items[ii + 1], base + ii + 1)
            if not nopost:
                emit_post(items[ii], base + ii)
        cnt[0] += n
        if lvl >= 3:
            break
    T.fence([("dr", "qk_a"), ("dr", "va")] + cast_keys + wq_keys)


def phase_a2(T, C):
    vec, cst, ones_f, ones_b = load_consts(T, C)
    BT = 2048
    qb = [T.sb(f"qb{i}", [128, BT], BF16) for i in range(2)]
    kb = [T.sb(f"kb{i}", [128, BT + 16 * 128], BF16) for i in range(2)]
    vb = [T.sb(f"vb{i}", [128, 32, 128], BF16) for i in range(2)]
    num = [T.sb(f"num{i}", [128, BT], F32) for i in range(2)]
    den = [T.sb(f"den{i}", [128, BT], F32) for i in range(2)]
    pT = [T.sb(f"pT{i}", [128, 512], BF16) for i in range(4)]
    ost = [T.sb(f"ost{i}", [128, BT], BF16) for i in range(2)]
    pS = [T.ps(f"pS{i}") for i in range(4)]
    pO = [T.ps(f"pO{i}") for i in range(2)]
    pD = [T.ps(f"pD{i}") for i in range(2)]
    for i in range(2):
        T.op("pool", lambda e, i=i: e.memset(kb[i][:], 0.0), w=[("kb", i)])
        T.op("pool", lambda e, i=i: e.memset(vb[i][:], 0.0), w=[("vb", i)])
    other_casts(T, C)
    scale = 1.0 / np.sqrt(128.0)
    steps = [(h, nb, g) for h in range(8) for nb in range(4) for g in range(3)]

    def views(si):
        h, nb, g = steps[si]
        i = si % 2
        r_ = DIL[g]
        W = BT // r_ + 128
        NT = (BT // r_) // 128 + 1
        qv = qb[i][:].rearrange("p (c m) -> p c m", c=r_)
        kv = kb[i][:, 0:r_ * W].rearrange("p (c w) -> p c w", c=r_)
        vv = vb[i][:, 0:r_ * NT, :].rearrange("p (c j) d -> p c j d", c=r_)
        return qv, kv, vv

    def load(si):
        h, nb, g = steps[si]
        i = si % 2
        r_ = DIL[g]
        L = S // r_
        n_ = BT // r_
        NT = n_ // 128 + 1
        m0 = nb * n_
        qv, kv, vv = views(si)
        qsrc = C.qk_a[0, g, h].rearrange("p (c l) -> p c l", c=r_)
        ksrc = C.qk_a[1, g, h].rearrange("p (c l) -> p c l", c=r_)
        dma(T, qv, qsrc[:, :, m0:m0 + n_], [("dr", "qk_a")], [("qb", i)], ("qb", i))
        lo = m0 - 64
        hi = m0 + n_ + 64
        clo = max(lo, 0)
        chi = min(hi, L)
        dma(T, kv[:, :, clo - lo:chi - lo], ksrc[:, :, clo:chi], [("dr", "qk_a")], [("kb", i)], ("kb", i))
        vsrc = C.va[g].rearrange("(c l) n -> c l n", c=r_)
        hc = slice(h * 128, (h + 1) * 128)
        jj0 = 1 if lo < 0 else 0
        jj1 = NT - 1 if hi > L else NT
        if r_ > 1:
            for c in range(r_):
                src = vsrc[c, lo + 128 * jj0:lo + 128 * jj1, hc].rearrange("(j i) d -> i j d", i=128)
                dma(T, vv[:, c, jj0:jj1, :], src, [("dr", "va")], [("vb", i)], ("vb", i))
        else:
            jm = (jj0 + jj1) // 2
            for a0, a1 in ((jj0, jm), (jm, jj1)):
                src = vsrc[0, lo + 128 * a0:lo + 128 * a1, hc].rearrange("(j i) d -> i j d", i=128)
                dma(T, vv[:, 0, a0:a1, :], src, [("dr", "va")], [("vb", i)], ("vb", i))
        if lo < 0:
            src = vsrc[:, 0:64, hc].rearrange("c i d -> i c d")
            dma(T, vv[64:128, :, 0, :], src, [("dr", "va")], [("vb", i)], ("vb", i))
        if hi > L:
            src = vsrc[:, L - 64:L, hc].rearrange("c i d -> i c d")
            dma(T, vv[0:64, :, NT - 1, :], src, [("dr", "va")], [("vb", i)], ("vb", i))

    pairs = []
    for si in range(len(steps)):
        for p in range(8):
            pairs.append((si, p))

    def chunk_info(si, ci):
        h, nb, g = steps[si]
        r_ = DIL[g]
        QC = (BT // r_) // 128
        c, a = divmod(ci, QC)
        var = 0
        if nb == 0 and a == 0:
            var = 1
        elif nb == 3 and a == QC - 1:
            var = 2
        return c, a, var

    loaded = [-1]

    def ensure_loaded(si):
        while loaded[0] < min(si + 1, len(steps) - 1):
            loaded[0] += 1
            load(loaded[0])

    def emit_s(pi):
        si, p = pairs[pi]
        i = si % 2
        qv, kv, vv = views(si)
        ps = pS[pi % 4]
        for s_ in range(2):
            c, a, var = chunk_info(si, 2 * p + s_)
            q_ap = qv[:, c, 128 * a:128 * a + 128]
            mm(T, ps[:, s_ * 256:s_ * 256 + 128], kv[:, c, 128 * a:128 * a + 128], q_ap, True, True,
               [("kb", i), ("qb", i)], [("pS", pi % 4)])
            mm(T, ps[:, s_ * 256 + 128:s_ * 256 + 256], kv[:, c, 128 * a + 128:128 * a + 256], q_ap, True, True,
               [("kb", i), ("qb", i)], [("pS", pi % 4)])

    def emit_rest(pi):
        si, p = pairs[pi]
        h, nb, g = steps[si]
        i = si % 2
        r_ = DIL[g]
        qv, kv, vv = views(si)
        ps = pS[pi % 4]
        pt = pT[pi % 4]
        ptk = ("pT", pi % 4)
        T.op("act", lambda e: e.activation(out=pt[:], in_=ps[:], func=AF.Exp, scale=float(scale)),
             r=[("pS", pi % 4)], w=[ptk])
        infos = [chunk_info(si, 2 * p + s_) for s_ in range(2)]
        if infos[0][2] == 0 and infos[1][2] == 0:
            T.op("dve", lambda e: e.tensor_tensor(out=pt[:], in0=pt[:], in1=cst[:, CB_MKN2:CB_MKN2 + 512], op=ALU.mult),
                 r=[ptk, "cst"], w=[ptk])
        else:
            for s_ in range(2):
                mo = CB_MK + 256 * infos[s_][2]
                T.op("dve", lambda e, s_=s_, mo=mo: e.tensor_tensor(out=pt[:, s_ * 256:(s_ + 1) * 256],
                                                                   in0=pt[:, s_ * 256:(s_ + 1) * 256],
                                                                   in1=cst[:, mo:mo + 256], op=ALU.mult),
                     r=[ptk, "cst"], w=[ptk])
        for s_ in range(2):
            ci = 2 * p + s_
            c, a, var = infos[s_]
            col = (ci % 4) * 128
            ob = (si * 4 + ci // 4) % 2
            po = pO[ob]
            pd = pD[ob]
            mm(T, po[:, col:col + 128], vv[:, c, a, :], pt[:, s_ * 256:s_ * 256 + 128], True, False,
               [("vb", i), ptk], [("pO", ob)])
            mm(T, po[:, col:col + 128], vv[:, c, a + 1, :], pt[:, s_ * 256 + 128:s_ * 256 + 256], False, True,
               [("vb", i), ptk], [("pO", ob)])
            mm(T, pd[:, col:col + 128], ones_b[:], pt[:, s_ * 256:s_ * 256 + 128], True, False,
               ["ones_b", ptk], [("pD", ob)])
            mm(T, pd[:, col:col + 128], ones_b[:], pt[:, s_ * 256 + 128:s_ * 256 + 256], False, True,
               ["ones_b", ptk], [("pD", ob)])
        if p % 2 == 1:
            ci0 = 2 * p - 2
            ob = (si * 4 + ci0 // 4) % 2
            nbuf = (h * 4 + nb) % 2
            QC = (BT // r_) // 128
            c0, a0 = divmod(ci0, QC)
            for (acc, pp, ak, pk) in ((num[nbuf], pO[ob], ("num", nbuf), ("pO", ob)),
                                      (den[nbuf], pD[ob], ("den", nbuf), ("pD", ob))):
                av = acc[:].rearrange("p (m c) -> p c m", c=r_)
                if r_ == 16:
                    dst = av[:, c0:c0 + 4, 0:128]
                    src = pp[:].rearrange("p (c m) -> p c m", c=4)
                else:
                    dst = av[:, c0, 128 * a0:128 * a0 + 512]
                    src = pp[:]
                if g == 0:
                    T.op("dve", lambda e, dst=dst, src=src: e.tensor_copy(out=dst, in_=src), r=[pk], w=[ak])
                else:
                    T.op("dve", lambda e, dst=dst, src=src: e.tensor_tensor(out=dst, in0=dst, in1=src, op=ALU.add),
                         r=[pk, ak], w=[ak])
        if g == 2 and p == 7:
            nbuf = (h * 4 + nb) % 2
            T.op("dve", lambda e: e.reciprocal(out=den[nbuf][:], in_=den[nbuf][:]), r=[("den", nbuf)], w=[("den", nbuf)])
            T.op("pool", lambda e: e.tensor_tensor(out=ost[nbuf][:], in0=num[nbuf][:], in1=den[nbuf][:], op=ALU.mult),
                 r=[("num", nbuf), ("den", nbuf)], w=[("ost", nbuf)])
            dma(T, C.oT[h * 128:(h + 1) * 128, nb * BT:(nb + 1) * BT], ost[nbuf][:], [("ost", nbuf)], [("dr", "oT")],
                ("ost", nbuf))

    npairs = len(pairs)
    LA = 3
    load(0)
    for pi in range(min(LA, npairs)):
        emit_s(pi)
    for pi in range(npairs):
        emit_rest(pi)
        si, p = pairs[pi]
        if p == 0 and si + 1 < len(steps):
            load(si + 1)
        if pi + LA < npairs:
            emit_s(pi + LA)
    T.fence([("dr", "oT")])


def phase_ffn(T, C, layer, xsrc, xdst, wo_b, xdst_key):
    TS = 1024
    vec, cst, ones_f, ones_b = load_consts(T, C)
    oTsb = [T.sb(f"oTs{i}", [128, 8, TS], BF16) for i in range(2)]
    xs = T.sb("xs", [128, 8, TS], F32)
    h2 = T.sb("h2", [128, 8, TS], BF16)
    aT = T.sb("aT", [128, 16, TS], BF16)
    sqx = [T.sb(f"sqx{i}", [128, TS], BF16) for i in range(2)]
    rr = T.sb("rr", [128, TS], F32)
    rl = [T.sb(f"rl{i}", [128, 512], F32) for i in range(2)]
    tmp = [T.sb(f"tmp{i}", [128, 512], F32) for i in range(2)]
    wb = [T.sb(f"wb{i}", [128, 8, 512], BF16) for i in range(3)]
    w2b = [T.sb(f"w2b{i}", [128, 16, 256], BF16) for i in range(2)]
    pA = [T.ps(f"pA{i}") for i in range(4)]
    pX = [T.ps(f"pX{i}") for i in range(2)]
    wo_v = wo_b.rearrange("(k p) n -> p k n", p=128)
    w1_v = C.w1_b[layer].rearrange("(k p) n -> p k n", p=128)
    w2_v = C.w2_b[layer].rearrange("(f p) n -> p f n", p=128)
    xsrc_v = xsrc.rearrange("(k p) t -> p k t", p=128)
    xdst_v = xdst.rearrange("(k p) t -> p k t", p=128)
    oT_v = C.oT.rearrange("(k p) t -> p k t", p=128)
    gcol0 = V_NFFN[layer]
    cnt = [0, 0, 0]

    def next_pa():
        i = cnt[0] % 4
        cnt[0] += 1
        return pA[i], ("pA", i)

    NSB = S // TS

    def load_oTs(sb_):
        i_ = sb_ % 2
        dma(T, oTsb[i_][:], oT_v[:, :, sb_ * TS:(sb_ + 1) * TS], [("dr", "oT")], [("oTs", i_)], ("oTs", i_))

    load_oTs(0)
    for sbi in range(NSB):
        t0 = sbi * TS
        oTs = oTsb[sbi % 2]
        oTk = ("oTs", sbi % 2)
        for kc in range(8):
            dma(T, xs[:, kc, :], xsrc_v[:, kc, t0:t0 + TS], [("dr", "xsrc")], [("xs", kc)], ("xsl", kc))
        if sbi + 1 < NSB:
            load_oTs(sbi + 1)
        for og in range(2):
            wi = cnt[1] % 3
            cnt[1] += 1
            dma(T, wb[wi][:], wo_v[:, :, og * 512:(og + 1) * 512], [("dr", "w")], [("wb", wi)], ("wb", wi))
            for ol in range(4):
                oc = og * 4 + ol
                for tb in range(2):
                    sl = slice(tb * 512, (tb + 1) * 512)
                    pa, pk = next_pa()
                    for kc in range(8):
                        mm(T, pa[:], wb[wi][:, kc, ol * 128:(ol + 1) * 128], oTs[:, kc, sl], kc == 0, kc == 7,
                           [("wb", wi), oTk], [pk])
                    T.op("dve", lambda e, pa=pa, oc=oc, sl=sl: e.tensor_tensor(out=xs[:, oc, sl], in0=xs[:, oc, sl],
                                                                                in1=pa[:], op=ALU.add),
                         r=[pk, ("xs", oc)], w=[("xs", oc)])
        for kc in range(8):
            b = kc % 2
            T.op("act", lambda e, b=b, kc=kc: e.activation(out=sqx[b][:], in_=xs[:, kc, :], func=AF.Square),
                 r=[("xs", kc)], w=[("sqx", b)])
            T.op("act", lambda e, kc=kc: e.activation(out=h2[:, kc, :], in_=xs[:, kc, :], func=AF.Copy,
                                                       scale=vec[:, gcol0 + kc:gcol0 + kc + 1]),
                 r=[("xs", kc), "vec"], w=["h2"])
            for tb in range(2):
                mm(T, pX[tb][:], ones_b[:], sqx[b][:, tb * 512:(tb + 1) * 512], kc == 0, kc == 7,
                   ["ones_b", ("sqx", b)], [("pX", tb)])
        for tb in range(2):
            sl = slice(tb * 512, (tb + 1) * 512)
            T.op("act", lambda e, tb=tb, sl=sl: e.activation(out=rr[:, sl], in_=pX[tb][:], func=AF.Ln, scale=1.0 / D, bias=EPS),
                 r=[("pX", tb)], w=["rr"])
        T.op("act", lambda e: e.activation(out=rr[:], in_=rr[:], func=AF.Exp, scale=-1.0), r=["rr"], w=["rr"])
        for hf in range(2):
            for fg in range(4):
                wi = cnt[1] % 3
                cnt[1] += 1
                c0 = hf * 2048 + fg * 512
                dma(T, wb[wi][:], w1_v[:, :, c0:c0 + 512], [("dr", "w")], [("wb", wi)], ("wb", wi))
                for fl in range(4):
                    f = fg * 4 + fl
                    for tb in range(2):
                        sl = slice(tb * 512, (tb + 1) * 512)
                        pa, pk = next_pa()
                        for kc in range(8):
                            mm(T, pa[:], wb[wi][:, kc, fl * 128:(fl + 1) * 128], h2[:, kc, sl], kc == 0, kc == 7,
                               [("wb", wi), "h2"], [pk])
                        j = cnt[0] % 2
                        T.op("act", lambda e, pa=pa, j=j: e.activation(out=rl[j][:], in_=pa[:], func=AF.Relu),
                             r=[pk], w=[("rl", j)])
                        T.op("pool", lambda e, j=j, f=f, sl=sl: e.tensor_tensor(out=aT[:, f, sl], in0=rl[j][:], in1=rl[j][:],
                                                                                op=ALU.mult),
                             r=[("rl", j)], w=["aT"])
            for o2 in range(4):
                wi = cnt[2] % 2
                cnt[2] += 1
                dma(T, w2b[wi][:], w2_v[:, hf * 16:(hf + 1) * 16, o2 * 256:(o2 + 1) * 256], [("dr", "w")], [("w2b", wi)],
                    ("w2b", wi))
                for ol in range(2):
                    oc = o2 * 2 + ol
                    for tb in range(2):
                        sl = slice(tb * 512, (tb + 1) * 512)
                        pa, pk = next_pa()
                        for f in range(16):
                            mm(T, pa[:], w2b[wi][:, f, ol * 128:(ol + 1) * 128], aT[:, f, sl], f == 0, f == 15,
                               [("w2b", wi), "aT"], [pk])
                        j = cnt[0] % 2
                        T.op("dve", lambda e, pa=pa, j=j, sl=sl: e.tensor_tensor(out=tmp[j][:], in0=pa[:], in1=rr[:, sl],
                                                                                op=ALU.mult),
                             r=[pk, "rr"], w=[("tmp", j)])
                        T.op("pool", lambda e, j=j, oc=oc, sl=sl: e.tensor_tensor(out=xs[:, oc, sl], in0=xs[:, oc, sl],
                                                                                  in1=tmp[j][:], op=ALU.add),
                             r=[("tmp", j), ("xs", oc)], w=[("xs", oc)])
                    if hf == 1:
                        dma(T, xdst_v[:, oc, t0:t0 + TS], xs[:, oc, :], [("xs", oc)], [(xdst_key, oc)], ("xss", oc), eng="act")
    T.fence([(xdst_key, oc) for oc in range(8)])


def phase_b1(T, C):
    TS = 1024
    vec, cst, ones_f, ones_b = load_consts(T, C)
    xin = [T.sb(f"xin{i}", [128, TS], F32) for i in range(2)]
    sqx = [T.sb(f"sqx{i}", [128, TS], BF16) for i in range(2)]
    hT = T.sb("hT", [128, 8, TS], BF16)
    rbc = T.sb("rbc", [128, TS], F32)
    cosb = T.sb("cosb", [64, TS], F32)
    sinb = T.sb("sinb", [64, TS], F32)
    win = T.sb("win", [128, 8, 448], BF16)
    wqb = T.sb("wqb", [128, 2, 1536], BF16)
    wkvb = T.sb("wkvb", [128, 2048], BF16)
    ucq = T.sb("ucq", [128, 2, TS], F32)
    uckv = T.sb("uckv", [128, TS], F32)
    cqn = T.sb("cqn", [128, 2, TS], BF16)
    ckvn = T.sb("ckvn", [128, TS], BF16)
    kraw = T.sb("kraw", [64, TS], F32)
    krr = T.sb("krr", [64, TS], F32)
    sqkr = T.sb("sqkr", [64, TS], BF16)
    sqb = [T.sb(f"sqb{i}", [128, 512], BF16) for i in range(2)]
    sqr = [T.sb(f"sqr{i}", [64, 512], BF16) for i in range(2)]
    gub = [T.sb(f"gub{i}", [64, 512], BF16) for i in range(2)]
    t1 = [T.sb(f"t1{i}", [64, 512], F32) for i in range(2)]
    t2 = [T.sb(f"t2{i}", [64, 512], F32) for i in range(2)]
    rs = [T.sb(f"rs{i}", [128, 512], F32) for i in range(2)]
    qno = [T.sb(f"qno{i}", [128, TS], BF16) for i in range(2)]
    qro = [T.sb(f"qro{i}", [64, TS], BF16) for i in range(2)]
    vst = [T.sb(f"vst{i}", [128, 512], BF16) for i in range(3)]
    pA = [T.ps(f"pA{i}") for i in range(2)]
    pB = [T.ps(f"pB{i}") for i in range(2)]
    pS = [T.ps(f"pS{i}") for i in range(2)]
    pR = [T.ps(f"pR{i}") for i in range(2)]
    stp = [(pS[0], ("pS", 0)), (pS[1], ("pS", 1))]
    dma(T, win[:], C.win_b.rearrange("(k p) n -> p k n", p=128), [("dr", "w")], ["win"], "win")
    dma(T, wqb[:], C.wqb_b.rearrange("(k p) n -> p k n", p=128), [("dr", "w")], ["wqb"], "wqb")
    dma(T, wkvb[:], C.wkvb_b, [("dr", "w")], ["wkvb"], "wkvb")
    pmb = cst[0:64, CB_PMB:CB_PMB + 64]
    cnt = [0, 0, 0]

    def nxt():
        j = cnt[0] % 2
        cnt[0] += 1
        return j

    for sbi in range(S // TS):
        t0 = sbi * TS
        norm_prologue(T, C, C.xmid, t0, TS, V_NMIX[1], xin, sqx, hT, rbc, stp, vec, ones_b)
        dma(T, cosb[:], C.cosB[:, t0:t0 + TS], [], ["cosb"], "cosb")
        dma(T, sinb[:], C.sinB[:, t0:t0 + TS], [], ["sinb"], "sinb")
        for tb in range(2):
            sl = slice(tb * 512, (tb + 1) * 512)
            for grp, chunks, nrm, gc0 in (("cq", (0, 1), 256.0, V_QA), ("ckv", (2,), 128.0, V_KVA)):
                js = nxt()
                for ci, oc in enumerate(chunks):
                    j = nxt()
                    for kc in range(8):
                        mm(T, pA[j][:], win[:, kc, oc * 128:(oc + 1) * 128], hT[:, kc, sl], kc == 0, kc == 7,
                           ["win", "hT"], [("pA", j)])
                    T.op("act", lambda e, j=j: e.activation(out=sqb[j][:], in_=pA[j][:], func=AF.Square),
                         r=[("pA", j)], w=[("sqb", j)])
                    udst = ucq[:, oc, sl] if grp == "cq" else uckv[:, sl]
                    T.op("dve", lambda e, j=j, udst=udst: e.tensor_copy(out=udst, in_=pA[j][:]), r=[("pA", j)], w=["u" + grp])
                    mm(T, pS[js][:], ones_b[:], sqb[j][:], ci == 0, ci == len(chunks) - 1, ["ones_b", ("sqb", j)],
                       [("pS", js)])
                rstd_from_ss(T, pS[js][:], ("pS", js), rs[js][:], ("rs", js), 1.0 / nrm, None, None)
                for ci, oc in enumerate(chunks):
                    usrc = ucq[:, oc, sl] if grp == "cq" else uckv[:, sl]
                    odst = cqn[:, oc, sl] if grp == "cq" else ckvn[:, sl]
                    T.op("dve", lambda e, usrc=usrc, odst=odst, ci=ci, js=js, gc0=gc0: e.scalar_tensor_tensor(
                        out=odst, in0=usrc, scalar=vec[:, gc0 + ci:gc0 + ci + 1], in1=rs[js][:], op0=ALU.mult, op1=ALU.mult),
                        r=["u" + grp, ("rs", js), "vec"], w=[grp + "n"])
            j = nxt()
            for kc in range(8):
                mm(T, pA[j][0:64, :], win[:, kc, 384:448], hT[:, kc, sl], kc == 0, kc == 7, ["win", "hT"], [("pA", j)])
            T.op("dve", lambda e, j=j, sl=sl: e.tensor_copy(out=kraw[:, sl], in_=pA[j][0:64, :]),
                 r=[("pA", j)], w=["kraw"])
            T.op("act", lambda e, sl=sl: e.activation(out=sqkr[:, sl], in_=kraw[:, sl], func=AF.Square),
                 r=["kraw"], w=["sqkr"])
            T.op("act", lambda e, j=j, sl=sl: e.activation(out=gub[j][:], in_=kraw[:, sl], func=AF.Copy,
                                                          scale=vec[0:64, V_KGR:V_KGR + 1]),
                 r=["kraw", "vec"], w=[("gub", j)])
            mm(T, pR[j][0:64, :], pmb, gub[j][:], True, True, ["cst", ("gub", j)], [("pR", j)])
            T.op("dve", lambda e, j=j, sl=sl: e.scalar_tensor_tensor(out=t1[j][:], in0=kraw[:, sl],
                                                                      scalar=vec[0:64, V_KGR:V_KGR + 1], in1=cosb[:, sl],
                                                                      op0=ALU.mult, op1=ALU.mult),
                 r=["kraw", "vec", "cosb"], w=[("t1", j)])
            T.op("dve", lambda e, j=j, sl=sl: e.tensor_tensor(out=t2[j][:], in0=pR[j][0:64, :], in1=sinb[:, sl], op=ALU.mult),
                 r=[("pR", j), "sinb"], w=[("t2", j)])
            T.op("dve", lambda e, j=j, sl=sl: e.tensor_tensor(out=krr[:, sl], in0=t1[j][:], in1=t2[j][:], op=ALU.add),
                 r=[("t1", j), ("t2", j)], w=["krr"])
        for h in range(8):
            oi = h % 2
            for tb in range(2):
                sl = slice(tb * 512, (tb + 1) * 512)
                j = nxt()
                for kc in range(2):
                    mm(T, pA[j][:], wqb[:, kc, h * 192:h * 192 + 128], cqn[:, kc, sl], kc == 0, kc == 1,
                       ["wqb", "cqn"], [("pA", j)])
                for kc in range(2):
                    mm(T, pB[j][0:64, :], wqb[:, kc, h * 192 + 128:h * 192 + 192], cqn[:, kc, sl], kc == 0, kc == 1,
                       ["wqb", "cqn"], [("pB", j)])
                T.op("act", lambda e, j=j: e.activation(out=sqb[j][:], in_=pA[j][:], func=AF.Square),
                     r=[("pA", j)], w=[("sqb", j)])
                T.op("act", lambda e, j=j: e.activation(out=sqr[j][:], in_=pB[j][0:64, :], func=AF.Square),
                     r=[("pB", j)], w=[("sqr", j)])
                mm(T, pS[j][:], ones_b[:], sqb[j][:], True, False, ["ones_b", ("sqb", j)], [("pS", j)])
                mm(T, pS[j][:], ones_b[0:64, :], sqr[j][:], False, True, ["ones_b", ("sqr", j)], [("pS", j)])
                rstd_from_ss(T, pS[j][:], ("pS", j), rs[j][:], ("rs", j), 1.0 / 192, None, None)
                T.op("dve", lambda e, j=j, sl=sl, oi=oi: e.scalar_tensor_tensor(
                    out=qno[oi][:, sl], in0=pA[j][:], scalar=vec[:, V_QG:V_QG + 1], in1=rs[j][:], op0=ALU.mult, op1=ALU.mult),
                    r=[("pA", j), ("rs", j), "vec"], w=[("qno", oi)])
                T.op("act", lambda e, j=j: e.activation(out=gub[j][:], in_=pB[j][0:64, :], func=AF.Copy,
                                                        scale=vec[0:64, V_QGR:V_QGR + 1]),
                     r=[("pB", j), "vec"], w=[("gub", j)])
                mm(T, pR[j][0:64, :], pmb, gub[j][:], True, True, ["cst", ("gub", j)], [("pR", j)])
                T.op("dve", lambda e, j=j, sl=sl: e.scalar_tensor_tensor(out=t1[j][:], in0=pB[j][0:64, :],
                                                                          scalar=vec[0:64, V_QGR:V_QGR + 1], in1=cosb[:, sl],
                                                                          op0=ALU.mult, op1=ALU.mult),
                     r=[("pB", j), "vec", "cosb"], w=[("t1", j)])
                T.op("dve", lambda e, j=j, sl=sl: e.tensor_tensor(out=t2[j][:], in0=pR[j][0:64, :], in1=sinb[:, sl],
                                                                   op=ALU.mult),
                     r=[("pR", j), "sinb"], w=[("t2", j)])
                T.op("dve", lambda e, j=j: e.tensor_tensor(out=t1[j][:], in0=t1[j][:], in1=t2[j][:], op=ALU.add),
                     r=[("t1", j), ("t2", j)], w=[("t1", j)])
                T.op("dve", lambda e, j=j, sl=sl, oi=oi: e.tensor_tensor(out=qro[oi][:, sl], in0=t1[j][:], in1=rs[j][0:64, :],
                                                                          op=ALU.mult),
                     r=[("t1", j), ("rs", j)], w=[("qro", oi)])
            dma(T, C.qn[h, :, t0:t0 + TS], qno[oi][:], [("qno", oi)], [("dr", "qn")], ("qno", oi))
            dma(T, C.qr[h, :, t0:t0 + TS], qro[oi][:], [("qro", oi)], [("dr", "qr")], ("qro", oi))
        for h in range(8):
            oi = h % 2
            for tb in range(2):
                sl = slice(tb * 512, (tb + 1) * 512)
                j = nxt()
                mm(T, pA[j][:], wkvb[:, h * 256:h * 256 + 128], ckvn[:, sl], True, True, ["wkvb", "ckvn"], [("pA", j)])
                T.op("act", lambda e, j=j: e.activation(out=sqb[j][:], in_=pA[j][:], func=AF.Square),
                     r=[("pA", j)], w=[("sqb", j)])
                mm(T, pS[j][:], ones_b[:], sqb[j][:], True, False, ["ones_b", ("sqb", j)], [("pS", j)])
                mm(T, pS[j][:], ones_b[0:64, :], sqkr[:, sl], False, True, ["ones_b", "sqkr"], [("pS", j)])
                rstd_from_ss(T, pS[j][:], ("pS", j), rs[j][:], ("rs", j), 1.0 / 192, None, None)
                T.op("dve", lambda e, j=j, sl=sl, oi=oi: e.scalar_tensor_tensor(
                    out=qno[oi][:, sl], in0=pA[j][:], scalar=vec[:, V_KG:V_KG + 1], in1=rs[j][:], op0=ALU.mult, op1=ALU.mult),
                    r=[("pA", j), ("rs", j), "vec"], w=[("qno", oi)])
                T.op("dve", lambda e, j=j, sl=sl, oi=oi: e.tensor_tensor(out=qro[oi][:, sl], in0=krr[:, sl], in1=rs[j][0:64, :],
                                                                          op=ALU.mult),
                     r=["krr", ("rs", j)], w=[("qro", oi)])
            dma(T, C.kn[h, :, t0:t0 + TS], qno[oi][:], [("qno", oi)], [("dr", "kn")], ("qno", oi))
            dma(T, C.kr[h, :, t0:t0 + TS], qro[oi][:], [("qro", oi)], [("dr", "kr")], ("qro", oi))
        wv = wkvb[:].rearrange("p (h x) -> p h x", x=256)
        for tt in range(TS // 128):
            for cbk in range(2):
                j = nxt()
                vi = cnt[2] % 3
                cnt[2] += 1
                mm(T, pA[j][:].rearrange("p (h x) -> p h x", x=128), ckvn[:, tt * 128:(tt + 1) * 128],
                   wv[:, cbk * 4:(cbk + 1) * 4, 128:256], True, True,
                   ["ckvn", "wkvb"], [("pA", j)])
                T.op("act", lambda e, j=j, vi=vi: e.activation(out=vst[vi][:], in_=pA[j][:], func=AF.Copy),
                     r=[("pA", j)], w=[("vst", vi)])
                dma(T, C.vbm[t0 + tt * 128:t0 + (tt + 1) * 128, cbk * 512:(cbk + 1) * 512], vst[vi][:], [("vst", vi)],
                    [("dr", "vbm")], ("vst", vi))
    T.fence([("dr", "qn"), ("dr", "qr"), ("dr", "kn"), ("dr", "kr"), ("dr", "vbm")])


def phase_b2(T, C):
    vec, cst, ones_f, ones_b = load_consts(T, C)
    knb = [T.sb(f"knb{i}", [128, S], BF16) for i in range(2)]
    krb = [T.sb(f"krb{i}", [128, S // 2], BF16) for i in range(2)]
    vb = [T.sb(f"vb{i}", [128, 64, 128], BF16) for i in range(2)]
    qnb = [T.sb(f"qnb{i}", [128, 512], BF16) for i in range(3)]
    qrb = [T.sb(f"qrb{i}", [128, 512], BF16) for i in range(3)]
    pT = [T.sb(f"pT{i}", [128, 512], BF16) for i in range(4)]
    rden = [T.sb(f"rden{i}", [128, 512], F32) for i in range(2)]
    ost = [T.sb(f"ost{i}", [128, 2048], BF16) for i in range(2)]
    pS = [T.ps(f"pS{i}") for i in range(4)]
    pO = [T.ps(f"pO{i}") for i in range(2)]
    pD = [T.ps(f"pD{i}") for i in range(2)]
    scale = 1.0 / np.sqrt(192.0)
    NQ = S // 512
    NK = S // 128

    def load_head(h):
        i = h % 2
        dma(T, knb[i][:], C.kn[h], [("dr", "kn")], [("knb", i)], ("knb", i))
        krsrc = C.kr[h].rearrange("p (j two c) -> p j two c", two=2, c=128)
        for half in range(2):
            dma(T, krb[i][half * 64:(half + 1) * 64, :].rearrange("p (j c) -> p j c", c=128), krsrc[:, :, half, :],
                [("dr", "kr")], [("krb", i)], ("krb", i))
        for q4 in range(4):
            src = C.vbm[q4 * 2048:(q4 + 1) * 2048, h * 128:(h + 1) * 128].rearrange("(j i) d -> i j d", i=128)
            dma(T, vb[i][:, q4 * 16:(q4 + 1) * 16, :], src, [("dr", "vbm")], [("vb", i)], ("vb", i))

    qlist = [(h, qb_) for h in range(8) for qb_ in range(NQ)]

    def load_q(qi):
        h, qb_ = qlist[qi]
        i = qi % 3
        dma(T, qnb[i][:], C.qn[h, :, qb_ * 512:(qb_ + 1) * 512], [("dr", "qn")], [("qnb", i)], ("qnb", i))
        for half in range(2):
            dma(T, qrb[i][half * 64:(half + 1) * 64, :], C.qr[h, :, qb_ * 512:(qb_ + 1) * 512], [("dr", "qr")],
                [("qrb", i)], ("qrb", i))

    tiles = [(qi, kt) for qi in range(len(qlist)) for kt in range(NK)]
    hl = [-1]
    ql = [-1]

    def emit_qk(tp):
        ti0 = 2 * tp
        qi, kt0 = tiles[ti0]
        h, qb_ = qlist[qi]
        while hl[0] < min(h + 1, 7) and (hl[0] < h or (qb_ >= NQ - 2)):
            hl[0] += 1
            load_head(hl[0])
        while ql[0] < min(qi + 1, len(qlist) - 1):
            ql[0] += 1
            load_q(ql[0])
        i = h % 2
        q3 = qi % 3
        for s_ in range(2):
            ti = ti0 + s_
            kt = kt0 + s_
            mm(T, pS[ti % 4][:], knb[i][:, kt * 128:(kt + 1) * 128], qnb[q3][:], True, False, [("knb", i), ("qnb", q3)],
               [("pS", ti % 4)])
        j = kt0 // 2
        for s_ in range(2):
            ti = ti0 + s_
            mm(T, pS[ti % 4][:], krb[i][s_ * 64:(s_ + 1) * 64, j * 128:(j + 1) * 128], qrb[q3][s_ * 64:(s_ + 1) * 64, :],
               False, True, [("krb", i), ("qrb", q3)], [("pS", ti % 4)])

    def emit_pv(ti):
        qi, kt = tiles[ti]
        h, qb_ = qlist[qi]
        i = h % 2
        ps = pS[ti % 4]
        pt = pT[ti % 4]
        ob = qi % 2
        T.op("act", lambda e: e.activation(out=pt[:], in_=ps[:], func=AF.Exp, scale=float(scale)),
             r=[("pS", ti % 4)], w=[("pT", ti % 4)])
        mm(T, pO[ob][:], vb[i][:, kt, :], pt[:], kt == 0, kt == NK - 1, [("vb", i), ("pT", ti % 4)], [("pO", ob)])
        mm(T, pD[ob][:], ones_b[:], pt[:], kt == 0, kt == NK - 1, ["ones_b", ("pT", ti % 4)], [("pD", ob)])
        if kt == NK - 1:
            oi = (qi // 4) % 2
            q4 = qi % 4
            T.op("dve", lambda e: e.reciprocal(out=rden[ob][:], in_=pD[ob][:]), r=[("pD", ob)], w=[("rden", ob)])
            T.op("dve", lambda e: e.tensor_tensor(out=ost[oi][:, q4 * 512:(q4 + 1) * 512], in0=pO[ob][:], in1=rden[ob][:],
                                                  op=ALU.mult),
                 r=[("pO", ob), ("rden", ob)], w=[("ost", oi)])
            if q4 == 3:
                b4 = qb_ // 4
                dma(T, C.oT[h * 128:(h + 1) * 128, b4 * 2048:(b4 + 1) * 2048], ost[oi][:], [("ost", oi)], [("dr", "oT")],
                    ("ost", oi))

    n = len(tiles)
    npair = n // 2
    emit_qk(0)
    for tp in range(npair):
        if tp + 1 < npair:
            emit_qk(tp + 1)
        emit_pv(2 * tp)
        emit_pv(2 * tp + 1)
    T.fence([("dr", "oT")])


def build_program(last_phase=7, debug=False, dbg_lvl=0):
    nc = bass.Bass("TRN2", target_bir_lowering=False)
    C = Ctx()
    C.dbg_lvl = dbg_lvl
    C.nc = nc

    def dr(name, shape, dt, kind="Internal"):
        return nc.dram_tensor(name, shape, dt, kind=kind).ap()

    ein = "ExternalInput"
    C.xT = dr("xT", [D, S], F32, ein)
    C.vecs = dr("vecs", [128, NV], F32, ein)
    C.cb16 = dr("cb16", [128, NCB], BF16, ein)
    C.cosA = dr("cosA", [128, S], F32, ein)
    C.sinA = dr("sinA", [128, S], F32, ein)
    C.cosB = dr("cosB", [64, S], F32, ein)
    C.sinB = dr("sinB", [64, S], F32, ein)
    C.a_w_qkv = dr("a_w_qkv", [D, 9216], F32, ein)
    a_w_o = dr("a_w_o", [D, D], F32, ein)
    b_w_in = dr("b_w_in", [D, 448], F32, ein)
    b_w_qb = dr("b_w_qb", [256, 1536], F32, ein)
    b_w_kvb = dr("b_w_kvb", [128, 2048], F32, ein)
    b_w_o = dr("b_w_o", [D, D], F32, ein)
    ffn_w1 = dr("ffn_w1", [2, D, 4096], F32, ein)
    ffn_w2 = dr("ffn_w2", [2, 4096, D], F32, ein)
    C.yT = dr("yT", [D, S], F32, "ExternalOutput")
    dbg_kind = "ExternalOutput" if debug else "Internal"
    C.xmid = dr("xmid", [D, S], F32, dbg_kind)
    C.oT = dr("oT", [D, S], BF16, dbg_kind)
    C.wqkv_b = dr("wqkv_b", [D, 9216], BF16)
    C.woa_b = dr("woa_b", [D, D], BF16)
    C.wob_b = dr("wob_b", [D, D], BF16)
    C.win_b = dr("win_b", [D, 448], BF16)
    C.wqb_b = dr("wqb_b", [256, 1536], BF16)
    C.wkvb_b = dr("wkvb_b", [128, 2048], BF16)
    C.w1_b = dr("w1_b", [2, D, 4096], BF16)
    C.w2_b = dr("w2_b", [2, 4096, D], BF16)
    C.qk_a = dr("qk_a", [2, 3, 8, 128, S], BF16)
    C.va = dr("va", [3, S, D], BF16)
    C.qn = dr("qn", [8, 128, S], BF16)
    C.qr = dr("qr", [8, 64, S], BF16)
    C.kn = dr("kn", [8, 128, S], BF16)
    C.kr = dr("kr", [8, 64, S], BF16)
    C.vbm = dr("vbm", [S, D], BF16)
    C.casts = [
        (C.woa_b, a_w_o, 1024, 512),
        (C.w1_b[0], ffn_w1[0], 1024, 256),
        (C.w2_b[0], ffn_w2[0], 4096, 1024),
        (C.win_b, b_w_in, 1024, 1024),
        (C.wqb_b, b_w_qb, 256, 256),
        (C.wkvb_b, b_w_kvb, 128, 128),
        (C.wob_b, b_w_o, 1024, 512),
        (C.w1_b[1], ffn_w1[1], 1024, 256),
        (C.w2_b[1], ffn_w2[1], 4096, 1024),
    ]
    phases = [
        lambda T: phase_cast(T, C),
        lambda T: phase_a1(T, C),
        lambda T: phase_a2(T, C),
        lambda T: phase_ffn(T, C, 0, C.xT, C.xmid, C.woa_b, ("dr", "xmid")),
        lambda T: phase_b1(T, C),
        lambda T: phase_b2(T, C),
        lambda T: phase_ffn(T, C, 1, C.xmid, C.yT, C.wob_b, ("dr", "yT")),
    ]
    for pi, ph in enumerate(phases):
        if pi >= last_phase:
            break
        run_phase(nc, ph, f"p{pi}_")
    return nc


def _rope_tables(d):
    pos = np.arange(S, dtype=np.float32)
    freqs = (np.float32(10000.0) ** (-np.arange(0, d, 2, dtype=np.float32) / np.float32(d))).astype(np.float32)
    ang = (pos[:, None] * freqs[None, :]).astype(np.float32)
    cos = np.concatenate([np.cos(ang), np.cos(ang)], -1).astype(np.float32)
    sin = np.concatenate([np.sin(ang), np.sin(ang)], -1).astype(np.float32)
    return np.ascontiguousarray(cos.T), np.ascontiguousarray(sin.T)


def _consts():
    cb = np.zeros((128, NCB), np.float32)
    for i in range(64):
        cb[i + 64, CB_PMA + i] = -1.0
        cb[i, CB_PMA + i + 64] = 1.0
    for i in range(32):
        cb[i + 32, CB_PMB + i] = -1.0
        cb[i, CB_PMB + i + 32] = 1.0
    kk = np.arange(128)[:, None]
    qq = np.arange(128)[None, :]
    mA = (qq <= kk).astype(np.float32)
    mB = (qq >= kk).astype(np.float32)
    normal = np.concatenate([mA, mB], 1)
    first = np.concatenate([mA * (kk >= 64), mB], 1)
    last = np.concatenate([mA, mB * (kk < 64)], 1)
    cb[:, CB_MK:CB_MK + 256] = normal
    cb[:, CB_MK + 256:CB_MK + 512] = first
    cb[:, CB_MK + 512:CB_MK + 768] = last
    cb[:, CB_MKN2:CB_MKN2 + 256] = normal
    cb[:, CB_MKN2 + 256:CB_MKN2 + 512] = normal
    return cb.astype(ml_dtypes.bfloat16)


def _pack_vecs(inp):
    v = np.zeros((128, NV), np.float32)
    for l in range(2):
        v[:, V_NMIX[l]:V_NMIX[l] + 8] = inp["norm_mix"][l].reshape(8, 128).T
        v[:, V_NFFN[l]:V_NFFN[l] + 8] = inp["norm_ffn"][l].reshape(8, 128).T
    for g in range(3):
        v[:, V_AQ + g] = inp["a_q_gain"][0, g]
        v[:, V_AK + g] = inp["a_k_gain"][0, g]
    v[:, V_QA:V_QA + 2] = inp["b_q_a_gain"][0].reshape(2, 128).T
    v[:, V_KVA] = inp["b_kv_a_gain"][0]
    v[:, V_QG] = inp["b_q_gain"][0, :128]
    v[:64, V_QGR] = inp["b_q_gain"][0, 128:]
    v[:, V_KG] = inp["b_k_gain"][0, :128]
    v[:64, V_KGR] = inp["b_k_gain"][0, 128:]
    v[0, V_E0] = 1.0
    return v


def make_in_maps(inp, cores):
    cosA, sinA = _rope_tables(128)
    cosB, sinB = _rope_tables(64)
    shared = {
        "vecs": _pack_vecs(inp), "cb16": _consts(), "cosA": cosA, "sinA": sinA, "cosB": cosB, "sinB": sinB,
        "a_w_qkv": np.ascontiguousarray(inp["a_w_qkv"][0]), "a_w_o": np.ascontiguousarray(inp["a_w_o"][0]),
        "b_w_in": np.ascontiguousarray(inp["b_w_in"][0]), "b_w_qb": np.ascontiguousarray(inp["b_w_qb"][0]),
        "b_w_kvb": np.ascontiguousarray(inp["b_w_kvb"][0]), "b_w_o": np.ascontiguousarray(inp["b_w_o"][0]),
        "ffn_w1": np.ascontiguousarray(inp["ffn_w1"]), "ffn_w2": np.ascontiguousarray(inp["ffn_w2"]),
    }
    maps = []
    for b in cores:
        m = dict(shared)
        m["xT"] = np.ascontiguousarray(np.asarray(inp["x"][b]).T)
        maps.append(m)
    return maps


def kernel(**inputs):
    inp = {k: np.asarray(v) for k, v in inputs.items()}
    nc = build_program()
    maps = make_in_maps(inp, list(range(NCORES)))
    res = run_bass_kernel_spmd(nc, maps, core_ids=list(range(NCORES)))
    out = np.empty((NCORES, S, D), np.float32)
    for b in range(NCORES):
        out[b] = np.asarray(res.results[b]["yT"]).T
    return out
```
